# Optimizing a Trainium2 kernel written in Bass

```python
import jax, jax.numpy as jnp
from jax import lax
import numpy as np

D_MODEL = 1024
BATCH = 4
SEQ = 4096
DEPTH = 2

GRID_W = 64
CTX_LEN = 256
N_MIXERS = 2
N_HEADS = 16
HEAD_DIM = D_MODEL // N_HEADS
WIN_ROWS = 8
WIN_COLS = 16
FOURIER_GROUPS = 4
GROUP_DIM = D_MODEL // FOURIER_GROUPS
D_FF = 4 * D_MODEL
N_ATTN_LAYERS = (DEPTH + 1) // 2
N_FOURIER_LAYERS = DEPTH // 2
EPS = 1e-6

kernel_name = "hybrid_natten_fnet_dit_block"


def _rmsnorm(x, g):
    x32 = x.astype(jnp.float32)
    y = x32 * lax.rsqrt(jnp.mean(x32 * x32, axis=-1, keepdims=True) + EPS)
    return (y * g.astype(jnp.float32)).astype(x.dtype)


def _modulate(h, shift, scale):
    return h * (1 + scale) + shift


def _squared_relu_mlp(h, w1, w2):
    return jnp.square(jax.nn.relu(h @ w1)) @ w2


def _fourier_mix(h, w):
    b, n, d = h.shape
    hg = h.astype(jnp.float32).reshape(b, n, FOURIER_GROUPS, GROUP_DIM)
    f = jnp.fft.fft2(hg, axes=(1, 3), norm="ortho").real
    return f.reshape(b, n, d).astype(h.dtype) @ w


def _neighbourhood_attention(h, hc, wqkv, wo, gq, gk, rpb, ctx_out):
    b, n, d = h.shape
    rows = n // GRID_W
    kh = min(WIN_ROWS, rows)
    scale = HEAD_DIM ** -0.5

    def proj(t):
        qkv = (t @ wqkv).reshape(t.shape[0], t.shape[1], 3, N_HEADS, HEAD_DIM)
        return _rmsnorm(qkv[:, :, 0], gq), _rmsnorm(qkv[:, :, 1], gk), qkv[:, :, 2]

    q, k, v = proj(h)
    qc, kc, vc = proj(hc)
    q_grid = q.reshape(b, rows, GRID_W, N_HEADS, HEAD_DIM)

    cols = jnp.arange(GRID_W)
    col_start = jnp.clip(cols - WIN_COLS // 2, 0, GRID_W - WIN_COLS)
    col_ok = (cols[None, :] >= col_start[:, None]) & (cols[None, :] < col_start[:, None] + WIN_COLS)
    dc_idx = jnp.clip(cols[None, :] - cols[:, None] + WIN_COLS - 1, 0, 2 * WIN_COLS - 2)

    def row_block(r):
        rs = jnp.clip(r - kh // 2, 0, rows - kh)
        k_win = lax.dynamic_slice_in_dim(k, rs * GRID_W, kh * GRID_W, axis=1).reshape(b, kh, GRID_W, N_HEADS, HEAD_DIM)
        v_win = lax.dynamic_slice_in_dim(v, rs * GRID_W, kh * GRID_W, axis=1).reshape(b, kh, GRID_W, N_HEADS, HEAD_DIM)
        q_r = lax.dynamic_index_in_dim(q_grid, r, axis=1, keepdims=False)
        dr = rs + jnp.arange(kh) - r
        bias = jnp.take(jnp.take(rpb, dr + WIN_ROWS - 1, axis=1), dc_idx, axis=2)
        bias = jnp.transpose(bias, (0, 2, 1, 3)).astype(jnp.float32)
        s_lat = jnp.einsum('bqhd,bikhd->bhqik', q_r, k_win).astype(jnp.float32) * scale + bias
        s_lat = jnp.where(col_ok[:, None, :], s_lat, -jnp.inf)
        s_ctx = jnp.einsum('bqhd,bkhd->bhqk', q_r, kc).astype(jnp.float32) * scale
        s = jnp.concatenate([s_lat.reshape(b, N_HEADS, GRID_W, kh * GRID_W), s_ctx], axis=-1)
        p = jax.nn.softmax(s, axis=-1).astype(v.dtype)
        p_lat = p[..., :kh * GRID_W].reshape(b, N_HEADS, GRID_W, kh, GRID_W)
        p_ctx = p[..., kh * GRID_W:]
        return (jnp.einsum('bhqik,bikhd->bqhd', p_lat, v_win)
                + jnp.einsum('bhqk,bkhd->bqhd', p_ctx, vc))

    o = lax.map(row_block, jnp.arange(rows))
    y = jnp.moveaxis(o, 0, 1).reshape(b, n, d) @ wo
    if not ctx_out:
        return y, None
    s_cc = jnp.einsum('bqhd,bkhd->bhqk', qc, kc).astype(jnp.float32) * scale
    p_cc = jax.nn.softmax(s_cc, axis=-1).astype(vc.dtype)
    oc = jnp.einsum('bhqk,bkhd->bqhd', p_cc, vc).reshape(hc.shape[0], hc.shape[1], d)
    return y, oc @ wo


def setup_inputs(seed: int = 0) -> dict:
    key = jax.random.key(seed)
    ks = jax.random.split(key, 15)
    nrm = jax.random.normal
    return {
        "x": nrm(ks[0], (BATCH, SEQ, D_MODEL), jnp.float32),
        "c": nrm(ks[1], (BATCH, D_MODEL), jnp.float32),
        "ctx": nrm(ks[2], (BATCH, CTX_LEN, D_MODEL), jnp.float32),
        "c_ctx": nrm(ks[3], (D_MODEL,), jnp.float32),
        "ada_w": nrm(ks[4], (DEPTH, D_MODEL, 6 * D_MODEL), jnp.float32) * D_MODEL ** -0.5,
        "ada_b": 0.02 * nrm(ks[5], (DEPTH, 6 * D_MODEL), jnp.float32),
        "norm_g": 1.0 + 0.05 * nrm(ks[6], (DEPTH, 2, D_MODEL), jnp.float32),
        "attn_wqkv": nrm(ks[7], (N_ATTN_LAYERS, D_MODEL, 3 * D_MODEL), jnp.float32) * D_MODEL ** -0.5,
        "attn_wo": nrm(ks[8], (N_ATTN_LAYERS, D_MODEL, D_MODEL), jnp.float32) * D_MODEL ** -0.5,
        "q_norm_g": 1.0 + 0.05 * nrm(ks[9], (N_ATTN_LAYERS, HEAD_DIM), jnp.float32),
        "k_norm_g": 1.0 + 0.05 * nrm(ks[10], (N_ATTN_LAYERS, HEAD_DIM), jnp.float32),
        "rpb": 0.1 * nrm(ks[11], (N_ATTN_LAYERS, N_HEADS, 2 * WIN_ROWS - 1, 2 * WIN_COLS - 1), jnp.float32),
        "fourier_w": nrm(ks[12], (N_FOURIER_LAYERS, D_MODEL, D_MODEL), jnp.float32) * D_MODEL ** -0.5,
        "mlp_w1": nrm(ks[13], (DEPTH, D_MODEL, D_FF), jnp.float32) * D_MODEL ** -0.5,
        "mlp_w2": nrm(ks[14], (DEPTH, D_FF, D_MODEL), jnp.float32) * D_FF ** -0.5,
    }


def reference(x, c, ctx, c_ctx, ada_w, ada_b, norm_g, attn_wqkv, attn_wo, q_norm_g, k_norm_g, rpb,
              fourier_w, mlp_w1, mlp_w2):
    for layer in range(DEPTH):
        is_attn = layer % N_MIXERS == 0
        idx = layer // N_MIXERS
        need_ctx = layer < DEPTH - 1
        uses_ctx = need_ctx or is_attn

        mods = jnp.split(jax.nn.silu(c) @ ada_w[layer] + ada_b[layer], 6, axis=-1)
        sh1, sc1, g1, sh2, sc2, g2 = [m[:, None, :] for m in mods]
        h = _modulate(_rmsnorm(x, norm_g[layer, 0]), sh1, sc1)
        if uses_ctx:
            csh1, csc1, cg1, csh2, csc2, cg2 = jnp.split(
                jax.nn.silu(c_ctx) @ ada_w[layer] + ada_b[layer], 6, axis=-1)
            hc = _modulate(_rmsnorm(ctx, norm_g[layer, 0]), csh1, csc1)

        if is_attn:
            y, yc = _neighbourhood_attention(h, hc, attn_wqkv[idx], attn_wo[idx], q_norm_g[idx],
                                             k_norm_g[idx], rpb[idx], need_ctx)
        else:
            y = _fourier_mix(h, fourier_w[idx])
            yc = _fourier_mix(hc, fourier_w[idx]) if need_ctx else None

        x = x + g1 * y
        x = x + g2 * _squared_relu_mlp(_modulate(_rmsnorm(x, norm_g[layer, 1]), sh2, sc2),
                                       mlp_w1[layer], mlp_w2[layer])
        if need_ctx:
            ctx = ctx + cg1 * yc
            ctx = ctx + cg2 * _squared_relu_mlp(_modulate(_rmsnorm(ctx, norm_g[layer, 1]), csh2, csc2),
                                               mlp_w1[layer], mlp_w2[layer])
    return x
```

```python
import numpy as np
import concourse.bass as bass
import concourse.mybir as mybir
from concourse.bass_utils import run_bass_kernel_spmd

F32 = mybir.dt.float32
BF16 = mybir.dt.bfloat16
U8 = mybir.dt.uint8
AF = mybir.ActivationFunctionType
ALU = mybir.AluOpType

D = 1024
NT = 2048
EPS = 1e-6
NEG = -30000.0


class Region:
    __slots__ = ("name", "w", "r")

    def __init__(self, name):
        self.name = name
        self.w = []
        self.r = []


class Sched:
    COMPUTE = ("pe", "act", "dve", "pool")

    def __init__(self, nc, sems, ndma=12):
        self.nc = nc
        self.ops = {e: [] for e in ("pe", "act", "dve", "pool", "sp")}
        self.sem = {e: sems[e] for e in self.COMPUTE}
        self.cnt = {e: 0 for e in self.COMPUTE}
        self.seen = {e: {} for e in self.ops}
        self.dsem = {"sp": sems["dsp"], "pool": sems["dpool"]}
        self.dctr = {"sp": 0, "pool": 0}
        self.dval = {"sp": [0] * len(sems["dsp"]), "pool": [0] * len(sems["dpool"])}
        self.extra = []

    def _waits(self, eng, toks):
        out = []
        seen = self.seen[eng]
        for (t, kind) in toks:
            key, val, src = t
            if src == eng and eng == "pe":
                continue
            if seen.get(key, 0) >= val:
                continue
            seen[key] = val
            out.append((key, val))
        return out

    def _deps(self, reads, writes):
        toks = []
        for r in reads:
            toks.extend((t, "raw") for t in r.w)
        for w in writes:
            toks.extend((t, "waw") for t in w.w)
            toks.extend((t, "war") for t in w.r)
        return toks

    def op(self, eng, fn, reads=(), writes=()):
        waits = self._waits(eng, self._deps(reads, writes))
        self.cnt[eng] += 1
        tok = (self.sem[eng], self.cnt[eng], eng)
        self.ops[eng].append((waits, fn, (self.sem[eng], 1)))
        for r in reads:
            r.r.append(tok)
        for w in writes:
            w.w = [tok]
            w.r = []
        return tok

    def group(self, eng, fns, reads=(), writes=()):
        waits = self._waits(eng, self._deps(reads, writes))
        self.cnt[eng] += 1
        tok = (self.sem[eng], self.cnt[eng], eng)
        for i, fn in enumerate(fns):
            self.ops[eng].append((waits if i == 0 else [], fn,
                                  (self.sem[eng], 1) if i == len(fns) - 1 else None))
        for r in reads:
            r.r.append(tok)
        for w in writes:
            w.w = [tok]
            w.r = []
        return tok

    def dma(self, q, fn, reads=(), writes=(), add=False):
        pool = self.dsem[q]
        i = self.dctr[q] % len(pool)
        self.dctr[q] += 1
        sem = pool[i]
        toks = self._deps(reads, writes)
        if add:
            toks = [(t, k) for (t, k) in toks if k != "waw"]
        prev = self.dval[q][i]
        if prev:
            toks.append(((sem, prev, "dma"), "raw"))
        waits = self._waits(q, toks)
        self.dval[q][i] = prev + 16
        tok = (sem, prev + 16, "dma")
        self.ops[q].append((waits, fn, (sem, 16)))
        for r in reads:
            r.r.append(tok)
        for w in writes:
            if add:
                w.w = w.w + [tok]
            else:
                w.w = [tok]
                w.r = []
        return tok

    def special(self, q, fn, sem, val, reads=(), writes=(), add=False):
        waits = self._waits(q, self._deps(reads, writes))
        tok = (sem, val, "dma")
        self.ops[q].append((waits, fn, (sem, 1)))
        self.extra.append(tok)
        for r in reads:
            r.r.append(tok)
        for w in writes:
            if add:
                w.w = w.w + [tok]
            else:
                w.w = [tok]
                w.r = []
        return tok

    def barrier(self):
        toks = [((self.sem[e], self.cnt[e], e), "raw") for e in self.COMPUTE if self.cnt[e]]
        for q in ("sp", "pool"):
            toks += [((self.dsem[q][i], v, "dma"), "raw") for i, v in enumerate(self.dval[q]) if v]
        toks += [(t, "raw") for t in self.extra]
        for eng in self.ops:
            self.ops[eng].append((self._waits(eng, toks), None, None))

    def emit(self):
        nc = self.nc
        with nc.Block() as block:
            def run(lst):
                def f(e):
                    for waits, fn, inc in lst:
                        for (s, v) in waits:
                            e.wait_ge(s, v)
                        if fn is None:
                            continue
                        ins = fn(e)
                        if inc is not None:
                            ins.then_inc(inc[0], inc[1])
                return f
            block.tensor(run(self.ops["pe"]))
            block.scalar(run(self.ops["act"]))
            block.vector(run(self.ops["dve"]))
            block.gpsimd(run(self.ops["pool"]))
            block.sync(run(self.ops["sp"]))


class Arena:
    def __init__(self, t, size):
        self.t = t
        self.size = size
        self.off = 0
        self.marks = []

    def alloc(self, shape, dt):
        esz = 2 if dt == BF16 else 4
        n = 1
        for s in shape[1:]:
            n *= s
        nbytes = (n * esz + 63) // 64 * 64
        assert self.off + nbytes <= self.size, f"arena overflow {self.off}+{nbytes}>{self.size}"
        ap = self.t[0:shape[0], self.off:self.off + n * esz].bitcast(dt)
        self.off += nbytes
        if len(shape) > 2:
            names = " ".join(f"d{i}" for i in range(1, len(shape)))
            kw = {f"d{i}": shape[i] for i in range(1, len(shape))}
            ap = ap.rearrange(f"p ({names}) -> p {names}", **kw)
        return ap

    def push(self):
        self.marks.append(self.off)

    def pop(self):
        self.off = self.marks.pop()


ARENA_BYTES = 188 * 1024


def build_program(mode="full"):
    nc = bass.Bass("TRN2", target_bir_lowering=False)

    def din(name, shape, dt=F32):
        return nc.dram_tensor(name, list(shape), dt, kind="ExternalInput").ap()

    xown = din("xown", [NT, D])
    xhalo = din("xhalo", [512, D])
    ctxb = din("ctxb", [256, D])
    smallv = din("smallv", [256, 128])
    ada_w = din("ada_w", [2, D, 6 * D])
    wqkv = din("wqkv", [D, 3 * D])
    wo = din("wo", [D, D])
    fw = din("fw", [D, D])
    w1 = din("w1", [2, D, 4 * D])
    w2 = din("w2", [2, 4 * D, D])
    btab = din("btab", [16, 128, 10 * 128])
    cmat = din("cmat", [128, 256])
    dftc = din("dftc", [256, 512])
    wsta = din("wsta", [128, 64])
    mbc = din("mbc", [64, 32 * 2 * 64])
    zin = nc.dram_tensor("zin", [2 * 64 * 8, 32 * 128], BF16)
    zout = nc.dram_tensor("zout", [2 * 2 * 64 * 8, 32 * 128], BF16)
    out = nc.dram_tensor("out", [NT, D], F32, kind="ExternalOutput").ap()

    import contextlib
    with contextlib.ExitStack() as es:
        arena_t = es.enter_context(nc.sbuf_tensor("arena", [128, ARENA_BYTES], U8))
        psum_t = es.enter_context(nc.psum_tensor("psum", [128, 8, 512], F32))
        sems = {}
        for e in ("pe", "act", "dve", "pool"):
            sems[e] = es.enter_context(nc.semaphore("s_" + e))
        sems["cc"] = [es.enter_context(nc.semaphore(f"s_cc{i}")) for i in range(16)]
        sems["dsp"] = [es.enter_context(nc.semaphore(f"dsp{i}")) for i in range(16)]
        sems["dpool"] = [es.enter_context(nc.semaphore(f"dpl{i}")) for i in range(16)]
        K = Sched(nc, sems)
        A = Arena(arena_t, ARENA_BYTES)

        pb = [Region(f"bank{i}") for i in range(8)]

        def bank(i):
            return psum_t[:, i, :]

        def bank2(i):
            return psum_t[:, i:i + 2, :].rearrange("p a c -> p (a c)")

        def bank_bf(i):
            return psum_t[:, i, :].bitcast(BF16)

        def lam(f, *a):
            return lambda e: f(e, *a)

        K.op("dve", lambda e: e.memset(psum_t[:, :, :], 0.0), writes=pb)
        ident_f = A.alloc([128, 128], F32)
        ident_b = A.alloc([128, 128], BF16)
        bones = A.alloc([128, 128], BF16)
        ones_b = A.alloc([128, 128], BF16)
        epsv = A.alloc([128, 2], F32)
        R_const = Region("const")
        R_c2 = Region("const2")
        K.dma("sp", lambda e: e.dma_start(out=ident_f, in_=cmat[:, 0:128]), writes=[R_const])
        K.dma("pool", lambda e: e.dma_start(out=ident_b, in_=cmat[:, 0:128]), writes=[R_const], add=True)
        K.dma("pool", lambda e: e.dma_start(out=bones, in_=cmat[:, 128:256]), writes=[R_const], add=True)
        K.op("pool", lambda e: e.memset(ones_b, 1.0 / 1024.0), writes=[R_c2])
        K.op("pool", lambda e: e.memset(epsv[:, 0:1], EPS), writes=[R_c2])
        K.op("pool", lambda e: e.memset(epsv[:, 1:2], 1024.0 * EPS), writes=[R_c2])

        TA = A.alloc([128, 128], F32)
        TB = A.alloc([128, 128], F32)
        sc = A.alloc([128, 2, 9, 8], F32)
        mods = A.alloc([128, 2, 6, 8, 2], F32)
        R_TA, R_TB, R_sc, R_mods = Region("TA"), Region("TB"), Region("sc"), Region("mods")
        cTs = A.alloc([128, 8, 2], BF16)
        R_cTs = Region("cTs")
        qg = A.alloc([128, 2], F32)
        R_qg = Region("qg")
        BASE = A.off
        OFF_R1, OFF_U, OFF_R3 = BASE, BASE + 32 * 1024, BASE + 96 * 1024
        A.off = OFF_U
        stA = A.alloc([128, 128], F32)
        stB = A.alloc([128, 128], F32)
        R_stA, R_stB = Region("stA"), Region("stB")
        K.dma("sp", lambda e: e.dma_start(out=stA, in_=smallv[0:128, :]), writes=[R_stA])
        K.dma("sp", lambda e: e.dma_start(out=stB, in_=smallv[128:256, :]), writes=[R_stB])
        K.op("pe", lambda e: e.transpose(bank(6)[:, 0:128], stA, ident_f), reads=[R_stA, R_const], writes=[pb[6]])
        K.op("dve", lambda e: e.tensor_copy(out=TA, in_=bank(6)[:, 0:128]), reads=[pb[6]], writes=[R_TA])
        K.op("pe", lambda e: e.transpose(bank(6)[:, 128:256], stB, ident_f), reads=[R_stB, R_const], writes=[pb[6]])
        K.op("dve", lambda e: e.tensor_copy(out=TB, in_=bank(6)[:, 128:256]), reads=[pb[6]], writes=[R_TB])
        gn = TB[:, 16:48].rearrange("p (l i k) -> p l i k", l=2, i=2)
        qkg_t = TB[:, 48:50]
        abT = TA[:, 0:96].rearrange("p (l vk) -> p l vk", l=2)
        K.op("act", lambda e: e.activation(out=cTs.rearrange("p k j -> p j k"),
                                           in_=TB[:, 0:16].rearrange("p (j k) -> p j k", j=2), func=AF.Silu),
             reads=[R_TB], writes=[R_cTs])
        K.op("dve", lambda e: e.scalar_tensor_tensor(out=qg[:, 0:1], in0=qkg_t[:, 0:1], scalar=0.125, in1=qkg_t[:, 1:2],
                                                     op0=ALU.mult, op1=ALU.mult), reads=[R_TB], writes=[R_qg])

        ADA_BLK = 768
        nblk = 6 * D // ADA_BLK
        A.off = OFF_R3
        awb = [A.alloc([128, 8, ADA_BLK], BF16) for _ in range(2)]
        R_awb = [Region("awb0"), Region("awb1")]
        OFF_R3_ATT = A.off
        modsv = mods.rearrange("p l v k j -> p l (v k) j")
        ada_ctr = {"n": 0}

        def ada_dma(l, blk):
            bi = ada_ctr["n"] % 2
            ada_ctr["n"] += 1
            src_ = ada_w[l].rearrange("(k p) f -> p k f", p=128)[:, :, blk * ADA_BLK:(blk + 1) * ADA_BLK]
            K.dma("pool", lambda e: e.dma_start(out=awb[bi], in_=src_), writes=[R_awb[bi]])
            return bi

        def ada_pe(l, blk, bi):
            nf = ADA_BLK // 128
            fns = []
            for fc in range(nf):
                for k in range(8):
                    fns.append(lam(lambda e, fc, k: e.matmul(
                        bank(7)[:, 2 * fc:2 * fc + 2], awb[bi][:, k, fc * 128:(fc + 1) * 128], cTs[:, k, :],
                        start=(k == 0), stop=(k == 7)), fc, k))
            K.group("pe", fns, reads=[R_awb[bi], R_cTs], writes=[pb[7]])
            for j in range(2):
                K.op("dve", lam(lambda e, j: e.tensor_tensor(
                    out=modsv[:, l, nf * blk:nf * blk + nf, j],
                    in0=bank(7)[:, 0:2 * nf].rearrange("p (f j) -> p f j", j=2)[:, :, j],
                    in1=abT[:, l, nf * blk:nf * blk + nf], op=ALU.add), j), reads=[pb[7], R_TA], writes=[R_mods])

        def sc_mul(l, i, v, j, ni, mul):
            K.op("dve", lambda e: e.scalar_tensor_tensor(
                out=sc[:, l, i, :], in0=mods[:, l, v, :, j], scalar=1.0, in1=gn[:, l, ni, :],
                op0=ALU.add, op1=ALU.mult), reads=[R_mods, R_TB], writes=[R_sc])
            if mul != 1.0:
                K.op("dve", lambda e: e.tensor_scalar(out=sc[:, l, i, :], in0=sc[:, l, i, :], scalar1=mul,
                                                      scalar2=None, op0=ALU.mult), reads=[R_sc], writes=[R_sc])

        def sc_cp(l, i, v, j):
            K.op("dve", lambda e: e.tensor_copy(out=sc[:, l, i, :], in_=mods[:, l, v, :, j]),
                 reads=[R_mods], writes=[R_sc])

        def sc_first(l):
            sc_mul(l, 0, 1, 0, 0, 32.0)
            sc_mul(l, 2, 1, 1, 0, 32.0)
            sc_cp(l, 1, 0, 0)
            sc_cp(l, 3, 0, 1)

        def sc_rest(l):
            sc_mul(l, 4, 4, 0, 1, 1.0)
            sc_mul(l, 8, 1, 0, 0, 1.0)
            for (i, v, j) in ((5, 3, 0), (6, 2, 0), (7, 5, 0)):
                sc_cp(l, i, v, j)

        for blk in range(3):
            ada_pe(0, blk, ada_dma(0, blk))
        sc_first(0)
        ada_todo = [(0, blk) for blk in range(3, nblk)] + [(1, blk) for blk in range(nblk)]

        A.off = OFF_U
        xT = A.alloc([128, 8, NT], F32)
        R_xT = [[Region(f"xT{k}_{tb}") for tb in range(4)] for k in range(8)]

        class WStream:
            def __init__(self, nbuf, shape, name):
                self.bufs = [A.alloc(shape, BF16) for _ in range(nbuf)]
                self.regs = [Region(f"{name}{i}") for i in range(nbuf)]
                self.n = 0

            def load(self, pairs_fn):
                i = self.n % len(self.bufs)
                self.n += 1
                buf, reg = self.bufs[i], self.regs[i]
                for j, (dst, src) in enumerate(pairs_fn(buf)):
                    K.dma("pool", lam(lambda e, dst, src: e.dma_start(out=dst, in_=src), dst, src),
                          writes=[reg], add=(j > 0))
                return buf, reg

        L = 0
        A.off = OFF_R1
        OT = A.alloc([128, 8, NT], BF16)
        R_OT = [[Region(f"OT{h}_{i}") for i in range(16)] for h in range(8)]
        A.off = OFF_U
        NTOK0 = NT + 512 + 256
        hT = A.alloc([128, 8, NTOK0], BF16)
        R_hT8 = [[Region(f"hT{t}_{k}") for k in range(8)] for t in range(22)]
        R_hT = None

        xin = [A.alloc([128, D], F32) for _ in range(3)]
        R_xin = [Region(f"xin{i}") for i in range(3)]
        xn = [A.alloc([128, D], BF16) for _ in range(2)]
        R_xn = [Region(f"xn{i}") for i in range(2)]
        junk = A.alloc([128, D], BF16)
        R_junk = Region("junk")
        ss = A.alloc([128, 8], F32)
        R_ss = [Region(f"ss{i}") for i in range(4)]

        def tile_src(t):
            if t < 16:
                return xown[t * 128:(t + 1) * 128, :]
            if t < 20:
                return xhalo[(t - 16) * 128:(t - 15) * 128, :]
            return ctxb[(t - 20) * 128:(t - 19) * 128, :]

        def phase0a_stats(t):
            xi, Rxi = xin[t % 3], R_xin[t % 3]
            s1 = ss[:, (t % 4) * 2:(t % 4) * 2 + 1]
            Rs = R_ss[t % 4]
            K.dma("sp", lambda e: e.dma_start(out=xi, in_=tile_src(t)), writes=[Rxi])
            K.op("act", lambda e: e.activation(out=junk, in_=xi, func=AF.Square, accum_out=s1),
                 reads=[Rxi], writes=[R_junk, Rs])

        def phase0a_tile(t):
            xi, Rxi = xin[t % 3], R_xin[t % 3]
            xo, Rxo = xn[t % 2], R_xn[t % 2]
            s1 = ss[:, (t % 4) * 2:(t % 4) * 2 + 1]
            s2 = ss[:, (t % 4) * 2 + 1:(t % 4) * 2 + 2]
            Rs = R_ss[t % 4]
            K.op("act", lambda e: e.activation(out=s2, in_=s1, func=AF.Ln, bias=epsv[:, 1:2]), reads=[Rs, R_c2], writes=[Rs])
            K.op("act", lambda e: e.activation(out=s1, in_=s2, func=AF.Exp, scale=-0.5), reads=[Rs], writes=[Rs])
            K.op("act", lambda e: e.activation(out=xo, in_=xi, func=AF.Identity, scale=s1),
                 reads=[Rxi, Rs], writes=[Rxo])
            bk = 6 + (t % 2)
            pbf = bank_bf(bk)
            K.group("pe", [lam(lambda e, k: e.transpose(pbf[:, k * 128:(k + 1) * 128], xo[:, k * 128:(k + 1) * 128], ident_b), k)
                           for k in range(8)], reads=[Rxo, R_const], writes=[pb[bk]])
            ia, ib = (0, 1) if t < 20 else (2, 3)
            for k in range(8):
                K.op("dve", lam(lambda e, k: e.tensor_scalar(
                    out=hT[:, k, t * 128:(t + 1) * 128], in0=pbf[:, k * 128:(k + 1) * 128],
                    scalar1=sc[:, L, ia, k:k + 1], scalar2=sc[:, L, ib, k:k + 1], op0=ALU.mult, op1=ALU.add), k),
                    reads=[pb[bk], R_sc], writes=[R_hT8[t][k]])

        phase0a_stats(0)
        for t in range(22):
            if t + 1 < 22:
                phase0a_stats(t + 1)
            phase0a_tile(t)
        assert A.off <= OFF_R3, A.off
        A.off = OFF_R3_ATT

        WQ = WStream(2, [128, 8, 384], "wqkv")
        QT = A.alloc([128, NT], BF16)
        KT = A.alloc([128, NTOK0], BF16)
        Vown = A.alloc([128, 16, 256], BF16)
        Vctx = A.alloc([128, 2, 256], BF16)
        Vhal = A.alloc([128, 16, 256], BF16)
        R_QT = [Region(f"QT{i}") for i in range(4)]
        R_KT = [Region(f"KT{i}") for i in range(6)]
        R_Vown = [Region(f"Vown{i}") for i in range(4)]
        R_Vctx = Region("Vctx")
        R_Vhal = [Region(f"Vhal{i}") for i in range(4)]
        R_ones = Region("vones")
        for V in (Vown, Vctx, Vhal):
            K.op("pool", lam(lambda e, V: e.memset(V[:, :, 64:192], 1.0), V), writes=[R_ones])
        sq = [A.alloc([128, 512], BF16) for _ in range(2)]
        R_sq = [Region("sq0"), Region("sq1")]
        tt = [A.alloc([128, 512], F32) for _ in range(2)]
        R_tt = [Region("tt0"), Region("tt1")]
        BT = [A.alloc([128, 10 * 128], BF16) for _ in range(2)]
        R_BT = [Region("BT0"), Region("BT1")]
        PT = [A.alloc([128, 768], BF16) for _ in range(4)]
        R_PT = [Region(f"PT{i}") for i in range(4)]
        rden = [A.alloc([128, 128], F32) for _ in range(2)]
        R_rden = [Region("rden0"), Region("rden1")]
        ctr = {"qk": 0, "pt": 0, "v": 0}

        def wq_src(hp):
            def f(buf):
                wv = wqkv.rearrange("(k p) f -> p k f", p=128)
                return [(buf[:, :, j * 128:(j + 1) * 128], wv[:, :, j * 1024 + hp * 128: j * 1024 + hp * 128 + 128])
                        for j in range(3)]
            return f

        def load_bt(hp, e2):
            K.dma("pool", lambda e: e.dma_start(out=BT[e2], in_=btab[2 * hp + e2]), writes=[R_BT[e2]])
            K.op("act", lambda e: e.activation(out=BT[e2], in_=BT[e2], func=AF.Exp), reads=[R_BT[e2]], writes=[R_BT[e2]])

        def proj_norm(wbuf, Rw, wcol, tok0, ntok, dst, Rdst, is_q):
            n = ctr["qk"]
            ctr["qk"] += 1
            pa, pn = 6 + (n % 2), 4 + (n % 2)
            sqb, Rsq = sq[n % 2], R_sq[n % 2]
            ttb, Rtt = tt[n % 2], R_tt[n % 2]
            tiles = sorted(set(range(tok0 // 128, (tok0 + ntok + 127) // 128)))
            K.group("pe", [lam(lambda e, k: e.matmul(bank(pa)[:, 0:ntok], wbuf[:, k, wcol * 128:(wcol + 1) * 128],
                                                      hT[:, k, tok0:tok0 + ntok], start=(k == 0), stop=(k == 7)), k)
                           for k in range(8)], reads=[Rw] + [R_hT8[t][k] for t in tiles for k in range(8)], writes=[pb[pa]])
            K.op("act", lambda e: e.activation(out=sqb[:, 0:ntok], in_=bank(pa)[:, 0:ntok], func=AF.Square),
                 reads=[pb[pa]], writes=[Rsq])
            K.op("pe", lambda e: e.matmul(bank(pn)[:, 0:ntok], bones, sqb[:, 0:ntok], start=True, stop=True),
                 reads=[Rsq, R_const], writes=[pb[pn]])
            K.op("act", lambda e: e.activation(out=ttb[:, 0:ntok], in_=bank(pn)[:, 0:ntok], func=AF.Ln, bias=epsv[:, 0:1]),
                 reads=[pb[pn], R_c2], writes=[Rtt])
            K.op("act", lambda e: e.activation(out=ttb[:, 0:ntok], in_=ttb[:, 0:ntok], func=AF.Exp, scale=-0.5),
                 reads=[Rtt], writes=[Rtt])
            scal = qg[:, 0:1] if is_q else 1.0
            K.op("dve", lambda e: e.scalar_tensor_tensor(out=dst, in0=bank(pa)[:, 0:ntok], scalar=scal,
                                                         in1=ttb[:, 0:ntok], op0=ALU.mult, op1=ALU.mult),
                 reads=[pb[pa], Rtt, R_qg], writes=[Rdst])

        VT = A.alloc([128, NTOK0], BF16)
        R_VT = [Region(f"VT{i}") for i in range(6)]

        def vT_proj(wbuf, Rw, tok0, ntok, Rdst):
            n = ctr["qk"]
            ctr["qk"] += 1
            pa = 6 + (n % 2)
            tiles = sorted(set(range(tok0 // 128, (tok0 + ntok + 127) // 128)))
            K.group("pe", [lam(lambda e, k: e.matmul(bank(pa)[:, 0:ntok], wbuf[:, k, 256:384],
                                                      hT[:, k, tok0:tok0 + ntok], start=(k == 0), stop=(k == 7)), k)
                           for k in range(8)], reads=[Rw] + [R_hT8[t][k] for t in tiles for k in range(8)], writes=[pb[pa]])
            if n % 2 == 0:
                K.op("act", lambda e: e.activation(out=VT[:, tok0:tok0 + ntok], in_=bank(pa)[:, 0:ntok], func=AF.Identity),
                     reads=[pb[pa]], writes=[Rdst])
            else:
                K.op("dve", lambda e: e.tensor_copy(out=VT[:, tok0:tok0 + ntok], in_=bank(pa)[:, 0:ntok]),
                     reads=[pb[pa]], writes=[Rdst])

        def vproj(tok_list, dstV, s0, Rdst):
            bk = 4 + ctr["v"] % 2
            ctr["v"] += 1
            pbf = bank_bf(bk)
            fns = []
            rd = set()
            for j, (t0, ln) in enumerate(tok_list):
                fns.append(lam(lambda e, j, t0, ln: e.transpose(pbf[0:ln, j * 128:(j + 1) * 128], VT[:, t0:t0 + ln], ident_b),
                               j, t0, ln))
                rd.add(t0 // 512 if t0 < 2560 else 5)
                rd.add((t0 + ln - 1) // 512 if t0 + ln - 1 < 2560 else 5)
            K.group("pe", fns, reads=[R_const] + [R_VT[r] for r in sorted(rd)], writes=[pb[bk]])
            n = len(tok_list)
            lnmax = max(ln for _, ln in tok_list)
            src_ = pbf[0:lnmax, 0:n * 128].rearrange("p (j c) -> p j c", c=128)
            K.op("act", lambda e: e.activation(out=dstV[0:lnmax, s0:s0 + n, 0:64], in_=src_[:, :, 0:64], func=AF.Identity),
                 reads=[pb[bk]], writes=[Rdst])
            K.op("dve", lambda e: e.tensor_copy(out=dstV[0:lnmax, s0:s0 + n, 192:256], in_=src_[:, :, 64:128]),
                 reads=[pb[bk], Rdst], writes=[Rdst])

        def halo_rng(i):
            r0 = max(4 * i - 4, 0)
            r1 = min(4 * i + 8, 64)
            return 2048 + r0 * 8, (r1 - r0) * 8

        def attn_A(hp, e2, i):
            p0 = 64 * e2
            own = [g for g in (i - 1, i, i + 1) if 0 <= g < 16]
            no = len(own)
            h0, hl = halo_rng(i)
            tb0 = 0 if i == 0 else (7 if i == 15 else 3)
            n = ctr["pt"]
            ctr["pt"] += 1
            sb = (0, 2, 6)[n % 3]
            ptb, Rpt = PT[n % 4], R_PT[n % 4]
            sps = bank2(sb)
            qs = QT[p0:p0 + 64, i * 128:(i + 1) * 128]
            chunks = [(g * 128, 128, "own", g) for g in own] + [(h0, hl, "hal", i)] + \
                     [(2560, 128, "ctx", 0), (2688, 128, "ctx", 1)]
            fns = []
            kr = set()
            for j, (t0, ln, _, _) in enumerate(chunks):
                fns.append(lam(lambda e, j, t0, ln: e.matmul(sps[0:ln, j * 128:(j + 1) * 128], KT[p0:p0 + 64, t0:t0 + ln],
                                                              qs, start=True, stop=True), j, t0, ln))
                kr.add(t0 // 512 if t0 < 2560 else 5)
            K.group("pe", fns, reads=[R_QT[i // 4]] + [R_KT[r] for r in sorted(kr)], writes=[pb[sb], pb[sb + 1]])
            ncol = (no + 3) * 128
            K.op("act", lambda e: e.activation(out=ptb[:, 0:ncol], in_=sps[:, 0:ncol], func=AF.Exp),
                 reads=[pb[sb], pb[sb + 1]], writes=[Rpt])
            nb = (no + 1) * 128
            K.op("dve", lambda e: e.tensor_tensor(out=ptb[:, 0:nb], in0=ptb[:, 0:nb],
                                                  in1=BT[e2][:, tb0 * 128:tb0 * 128 + nb], op=ALU.mult),
                 reads=[Rpt, R_BT[e2]], writes=[Rpt])
            return (hp, e2, i, n, chunks, ptb, Rpt)

        def attn_B(st):
            hp, e2, i, n, chunks, ptb, Rpt = st
            p0 = 64 * e2
            ob = 4 + (n % 2)
            fns = []
            vr = []
            for j, (t0, ln, kind, idx) in enumerate(chunks):
                if kind == "own":
                    vt, rr = Vown[:, idx, :], R_Vown[idx // 4]
                elif kind == "hal":
                    vt, rr = Vhal[:, idx, :], R_Vhal[idx // 4]
                else:
                    vt, rr = Vctx[:, idx, :], R_Vctx
                vr.append(rr)
                fns.append(lam(lambda e, j, ln, vt: e.matmul(
                    bank(ob)[:, 0:128], vt[0:ln, 128 * e2:128 * e2 + 128], ptb[0:ln, j * 128:(j + 1) * 128],
                    start=(j == 0), stop=(j == len(chunks) - 1)), j, ln, vt))
            K.group("pe", fns, reads=[Rpt, R_ones] + vr, writes=[pb[ob]])
            rd, Rrd = rden[n % 2], R_rden[n % 2]
            dn = 64 - p0
            K.op("act", lambda e: e.activation(out=rd[dn:dn + 64, :], in_=bank(ob)[dn:dn + 64, 0:128], func=AF.Ln),
                 reads=[pb[ob]], writes=[Rrd])
            K.op("act", lambda e: e.activation(out=rd[dn:dn + 64, :], in_=rd[dn:dn + 64, :], func=AF.Exp, scale=-1.0),
                 reads=[Rrd], writes=[Rrd])
            K.op("dve", lambda e: e.tensor_copy(out=rd[p0:p0 + 64, :], in_=rd[dn:dn + 64, :]), reads=[Rrd], writes=[Rrd])
            K.op("dve", lambda e: e.tensor_tensor(out=OT[p0:p0 + 64, hp, i * 128:(i + 1) * 128],
                                                  in0=bank(ob)[p0:p0 + 64, 0:128], in1=rd[p0:p0 + 64, :], op=ALU.mult),
                 reads=[pb[ob], Rrd], writes=[R_OT[hp][i]])

        nxt = WQ.load(wq_src(0))
        NHP = 8
        for hp in range(NHP):
            wbuf, Rw = nxt
            if hp + 1 < NHP:
                nxt = WQ.load(wq_src(hp + 1))
            for e2 in range(2):
                load_bt(hp, e2)
            ada_now = [(lb, ada_dma(*lb)) for lb in ada_todo[2 * hp:2 * hp + 2]]
            for tb in range(4):
                proj_norm(wbuf, Rw, 0, tb * 512, 512, QT[:, tb * 512:(tb + 1) * 512], R_QT[tb], True)
            for tb in range(4):
                proj_norm(wbuf, Rw, 1, tb * 512, 512, KT[:, tb * 512:(tb + 1) * 512], R_KT[tb], False)
            proj_norm(wbuf, Rw, 1, 2048, 512, KT[:, 2048:2560], R_KT[4], False)
            proj_norm(wbuf, Rw, 1, 2560, 256, KT[:, 2560:2816], R_KT[5], False)
            for tb in range(4):
                vT_proj(wbuf, Rw, tb * 512, 512, R_VT[tb])
            vT_proj(wbuf, Rw, 2048, 512, R_VT[4])
            vT_proj(wbuf, Rw, 2560, 256, R_VT[5])
            for g4 in range(4):
                vproj([(128 * (4 * g4 + j), 128) for j in range(4)], Vown, 4 * g4, R_Vown[g4])
            vproj([(2560, 128), (2688, 128)], Vctx, 0, R_Vctx)
            for g4 in range(4):
                vproj([halo_rng(4 * g4 + j) for j in range(4)], Vhal, 4 * g4, R_Vhal[g4])
            pend = []
            for e2 in range(2):
                for i in range(16):
                    pend.append(attn_A(hp, e2, i))
                    if len(pend) > 2:
                        attn_B(pend.pop(0))
            while pend:
                attn_B(pend.pop(0))
            for (lb, bi) in ada_now:
                ada_pe(lb[0], lb[1], bi)
        assert 2 * NHP >= len(ada_todo)
        sc_rest(0)
        sc_first(1)
        sc_rest(1)
        K.barrier()
        A.off = OFF_R3

        xin2 = [A.alloc([128, D], F32) for _ in range(3)]
        R_xin2 = [Region(f"xinb{i}") for i in range(3)]
        WO = WStream(1, [128, 8, D], "wo")
        wob, Rwo = WO.load(lambda buf: [(buf, wo.rearrange("(k p) f -> p k f", p=128))])

        def load_xT(t):
            xi, Rxi = xin2[t % 3], R_xin2[t % 3]
            K.dma("sp", lambda e: e.dma_start(out=xi, in_=xown[t * 128:(t + 1) * 128, :]), writes=[Rxi])
            b0 = 4 + 2 * (t % 2)
            pst = bank2(b0)
            K.group("pe", [lam(lambda e, k: e.transpose(pst[:, k * 128:(k + 1) * 128], xi[:, k * 128:(k + 1) * 128], ident_f), k)
                           for k in range(8)], reads=[Rxi, R_const], writes=[pb[b0], pb[b0 + 1]])
            K.op("act", lambda e: e.activation(out=xT[:, :, t * 128:(t + 1) * 128],
                                               in_=pst.rearrange("p (k c) -> p k c", c=128), func=AF.Identity),
                 reads=[pb[b0], pb[b0 + 1]], writes=[R_xT[k][t // 4] for k in range(8)])

        for t in range(16):
            load_xT(t)
        cw = {"n": 0}

        def wo_block(tb, oc):
            bk = cw["n"] % 4
            cw["n"] += 1
            K.group("pe", [lam(lambda e, hp: e.matmul(bank(bk), wob[:, hp, oc * 128:(oc + 1) * 128],
                                                       OT[:, hp, tb * 512:(tb + 1) * 512], start=(hp == 0), stop=(hp == 7)), hp)
                           for hp in range(8)],
                    reads=[Rwo] + [R_OT[hp][i] for hp in range(8) for i in range(4 * tb, 4 * tb + 4)], writes=[pb[bk]])
            K.op("dve", lambda e: e.scalar_tensor_tensor(
                out=xT[:, oc, tb * 512:(tb + 1) * 512], in0=bank(bk), scalar=sc[:, L, 6, oc:oc + 1],
                in1=xT[:, oc, tb * 512:(tb + 1) * 512], op0=ALU.mult, op1=ALU.add),
                reads=[pb[bk], R_sc, R_xT[oc][tb]], writes=[R_xT[oc][tb]])

        for tb in range(4):
            for oc in range(8):
                wo_block(tb, oc)
        K.barrier()

        def norm_mod(l, ia, ib, dst, R_dst, after_blk=None, bufs=None, run=True):
            if bufs is None:
                bufs = ([A.alloc([128, 512], BF16) for _ in range(2)], [Region("msq0"), Region("msq1")],
                        [A.alloc([128, 512], F32) for _ in range(2)], [Region("mtt0"), Region("mtt1")],
                        [A.alloc([128, 512], F32) for _ in range(2)], [Region("mtmp0"), Region("mtmp1")])
            sqs, R_sqs, tts, R_tts, tmp, R_tmp = bufs

            def blk(tb):
                tsl = slice(tb * 512, (tb + 1) * 512)
                pn = 4 + tb % 2
                for k in range(8):
                    K.op("act", lam(lambda e, k: e.activation(out=sqs[k % 2], in_=xT[:, k, tsl], func=AF.Square), k),
                         reads=[R_xT[k][tb]], writes=[R_sqs[k % 2]])
                    K.op("pe", lam(lambda e, k: e.matmul(bank(pn), ones_b, sqs[k % 2], start=(k == 0), stop=(k == 7)), k),
                         reads=[R_sqs[k % 2], R_c2], writes=[pb[pn]])
                ttb, Rtt = tts[tb % 2], R_tts[tb % 2]
                K.op("act", lambda e: e.activation(out=ttb, in_=bank(pn), func=AF.Ln, bias=epsv[:, 0:1]),
                     reads=[pb[pn], R_c2], writes=[Rtt])
                K.op("act", lambda e: e.activation(out=ttb, in_=ttb, func=AF.Exp, scale=-0.5), reads=[Rtt], writes=[Rtt])
                for k in range(8):
                    tm, Rtm = tmp[k % 2], R_tmp[k % 2]
                    K.op("dve", lam(lambda e, k, tm: e.tensor_tensor(out=tm, in0=xT[:, k, tsl], in1=ttb, op=ALU.mult), k, tm),
                         reads=[R_xT[k][tb], Rtt], writes=[Rtm])
                    K.op("act", lam(lambda e, k, tm: e.activation(out=dst[:, k, tsl], in_=tm, func=AF.Identity,
                                                                   scale=sc[:, l, ia, k:k + 1], bias=sc[:, l, ib, k:k + 1]), k, tm),
                         reads=[Rtm, R_sc], writes=[R_dst[k][tb]])
            if run:
                for tb in range(4):
                    blk(tb)
                    if after_blk is not None:
                        after_blk(tb)
            return blk, bufs

        def mlp(l, final_store=None, tail_setup=None):
            A.off = OFF_R1
            h2 = A.alloc([128, 8, NT], BF16)
            R_h2 = [[Region(f"h2_{k}_{tb}") for tb in range(4)] for k in range(8)]
            A.off = OFF_R3
            uT = A.alloc([128, 8, NT], BF16)
            R_uT = [[Region(f"uT{c}_{tb}") for tb in range(4)] for c in range(8)]
            w1_off = A.off
            W1 = WStream(4, [128, 8, 256], "w1s")
            W2 = WStream(5, [128, 2, D], "w2s")
            tail_off = A.off
            w1v = w1[l].rearrange("(k p) f -> p k f", p=128)
            w2v = w2[l].rearrange("(c p) f -> p c f", p=128)
            cn = {"a": 0, "b": 0}

            def up(w1b, Rw1, cj, c, tb):
                bk = cn["a"] % 4
                cn["a"] += 1
                K.group("pe", [lam(lambda e, k: e.matmul(bank(bk), w1b[:, k, cj * 128:(cj + 1) * 128],
                                                          h2[:, k, tb * 512:(tb + 1) * 512], start=(k == 0), stop=(k == 7)), k)
                               for k in range(8)], reads=[Rw1] + [R_h2[k][tb] for k in range(8)], writes=[pb[bk]])
                usl = uT[:, c, tb * 512:(tb + 1) * 512]
                K.op("act", lambda e: e.activation(out=usl, in_=bank(bk), func=AF.Relu), reads=[pb[bk]], writes=[R_uT[c][tb]])
                sq_eng = "dve"
                K.op(sq_eng, lambda e: e.tensor_tensor(out=usl, in0=usl, in1=usl, op=ALU.mult),
                     reads=[R_uT[c][tb]], writes=[R_uT[c][tb]])

            def down(w2l, oc, tb):
                bk = 4 + cn["b"] % 4
                cn["b"] += 1
                K.group("pe", [lam(lambda e, c: e.matmul(bank(bk), w2l[c // 2][0][:, c % 2, oc * 128:(oc + 1) * 128],
                                                          uT[:, c, tb * 512:(tb + 1) * 512], start=(c == 0), stop=(c == 7)), c)
                               for c in range(8)],
                        reads=[w2l[j][1] for j in range(4)] + [R_uT[c][tb] for c in range(8)], writes=[pb[bk]])
                K.op("dve", lambda e: e.scalar_tensor_tensor(
                    out=xT[:, oc, tb * 512:(tb + 1) * 512], in0=bank(bk), scalar=sc[:, l, 7, oc:oc + 1],
                    in1=xT[:, oc, tb * 512:(tb + 1) * 512], op0=ALU.mult, op1=ALU.add),
                    reads=[pb[bk], R_sc, R_xT[oc][tb]], writes=[R_xT[oc][tb]])

            w1q0 = [W1.load(lambda buf, c2=c2: [(buf, w1v[:, :, c2 * 256:(c2 + 1) * 256])]) for c2 in range(4)]

            def q0_block(tb):
                for c2 in range(4):
                    for cj in range(2):
                        up(w1q0[c2][0], w1q0[c2][1], cj, c2 * 2 + cj, tb)
            _, nbufs = norm_mod(l, 4, 5, h2, R_h2, after_blk=q0_block)
            for q in range(4):
                blockwise = (q == 3 and tail_setup is not None)
                w1q = []
                for c2 in range(4):
                    if q == 0:
                        continue
                    f0 = q * 1024 + c2 * 256
                    w1b, Rw1 = W1.load(lambda buf: [(buf, w1v[:, :, f0:f0 + 256])])
                    w1q.append((w1b, Rw1))
                    if blockwise:
                        continue
                    for cj in range(2):
                        for tb in range(4):
                            up(w1b, Rw1, cj, c2 * 2 + cj, tb)
                w2l = []
                for c2 in range(4):
                    cc0 = q * 8 + c2 * 2
                    w2l.append(W2.load(lambda buf: [(buf, w2v[:, cc0:cc0 + 2, :])]))
                if blockwise:
                    tail_setup(A.off, [], nbufs, h2, R_h2)
                    for tb in range(4):
                        for c2 in range(4):
                            for cj in range(2):
                                up(w1q[c2][0], w1q[c2][1], cj, c2 * 2 + cj, tb)
                        for oc in range(8):
                            down(w2l, oc, tb)
                        final_store(tb)
                    continue
                if q == 3 and final_store is not None:
                    for tb in range(4):
                        for oc in range(8):
                            down(w2l, oc, tb)
                        final_store(tb)
                else:
                    for oc in range(8):
                        for tb in range(4):
                            down(w2l, oc, tb)
            K.barrier()

        L1MODES = ("l1mix", "full", "nocc")


        l1 = {}

        def l1_front_setup(w1_off, w1_regs, nbufs, h2, R_h2):
            l1["h1"], l1["R_h1"] = h2, R_h2
            _save = A.off
            A.off = w1_off
            l1["F256"] = A.alloc([128, 2, 512], BF16)
            l1["Zt"] = [A.alloc([128, 2, D], BF16) for _ in range(2)]
            A.off = _save
            l1["R_kf"] = Region("l1F256")
            l1["R_Zt"] = [Region("Zt0"), Region("Zt1")]
            l1["w1_regs"] = w1_regs
            K.dma("pool", lambda e: e.dma_start(out=l1["F256"], in_=dftc.rearrange("(k p) f -> p k f", p=128)),
                  writes=[l1["R_kf"]] + list(w1_regs))
            l1["R_zin"] = [Region(f"zin{t}") for t in range(16)]
            l1["R_zout"] = Region("zout")
            l1["cn"] = 0
            l1["blk"], _ = norm_mod(1, 8, 1, l1["h1"], l1["R_h1"], bufs=nbufs, run=False)

        def l1_front_blk(tb):
            l = 1
            h1, R_h1, F256, Zt, R_Zt = l1["h1"], l1["R_h1"], l1["F256"], l1["Zt"], l1["R_Zt"]
            R_zin = l1["R_zin"]
            zin_v = zin.ap().rearrange("(n c b) (m l) -> c n m b l", c=2, n=64, b=8, m=32, l=128)
            l1["blk"](tb)

            def chan_dft(t):
                zt, Rzt = Zt[t % 2], R_Zt[t % 2]
                extra = list(l1["w1_regs"]) if t < 2 else []

                def grp(g):
                    bk = l1["cn"] % 4
                    l1["cn"] += 1
                    K.group("pe", [lam(lambda e, kk: e.matmul(bank(bk), h1[:, 2 * g + kk, t * 128:(t + 1) * 128], F256[:, kk, :],
                                                               start=(kk == 0), stop=(kk == 1)), kk) for kk in range(2)],
                            reads=[l1["R_kf"]] + [R_h1[2 * g + kk][t // 4] for kk in range(2)], writes=[pb[bk]])
                    src_ = bank(bk).rearrange("p (c m) -> p c m", c=2)
                    dst = zt[:, :, g * 256:(g + 1) * 256]
                    if g % 2 == 0:
                        K.op("act", lambda e: e.activation(out=dst, in_=src_, func=AF.Identity), reads=[pb[bk]],
                             writes=[Rzt] + extra)
                    else:
                        K.op("dve", lambda e: e.tensor_copy(out=dst, in_=src_), reads=[pb[bk], Rzt], writes=[Rzt] + extra)
                for g in range(4):
                    grp(g)
                for c in range(2):
                    for rl in range(4):
                        K.dma("sp", lam(lambda e, c, rl: e.dma_start(
                            out=zin_v[c, 4 * t + rl],
                            in_=zt[rl * 32:(rl + 1) * 32, c, :].rearrange("p (b l) -> p b l", l=128)), c, rl),
                            reads=[Rzt], writes=[R_zin[t]], add=True)

            for t in range(4 * tb, 4 * tb + 4):
                chan_dft(t)
            K.special("pool", lambda e: e.collective_compute(
                "AllGather", ALU.bypass, replica_groups=[[0, 1], [2, 3], [4, 5], [6, 7]],
                ins=[zin.ap()[256 * tb:256 * (tb + 1), :]], outs=[zout.ap()[512 * tb:512 * (tb + 1), :]]),
                sems["cc"][tb], 1, reads=[R_zin[t] for t in range(4 * tb, 4 * tb + 4)], writes=[l1["R_zout"]], add=True)

        def fourier_back():
            l = 1
            R_zout = l1["R_zout"]
            A.off = OFF_R3
            WA = A.alloc([128, 64], BF16)
            MB = A.alloc([128, 32, 2, 64], BF16)
            FWP = A.alloc([128, 8, D], BF16)
            L1C_END = A.off
            R_k = Region("l1consts")
            K.dma("pool", lambda e: e.dma_start(out=WA, in_=wsta), writes=[R_k])
            K.dma("pool", lambda e: e.dma_start(out=MB[0:64], in_=mbc.rearrange("p (a b c) -> p a b c", a=32, b=2)),
                  writes=[R_k], add=True)
            K.dma("pool", lambda e: e.dma_start(out=FWP, in_=fw.rearrange("(k p) f -> p k f", p=128)),
                  writes=[R_k], add=True)
            A.off = OFF_R1
            FT = A.alloc([128, 8, NT], BF16)
            R_FT = [Region(f"FT{c}") for c in range(8)]
            A.off = L1C_END
            Zb = [A.alloc([128, 64, 128], BF16) for _ in range(2)]
            R_Zb = [Region("Zb0"), Region("Zb1")]
            Ab = [A.alloc([128, 128, 64], BF16)]
            R_Ab = [Region("Ab0")]
            zout_v = zout.ap().rearrange("(g r t n c b) (m l) -> r c b g t n m l", g=4, r=2, t=4, n=4, c=2, b=8, m=32, l=128)

            def pos_dft(cb):
                zb, Rzb = Zb[cb % 2], R_Zb[cb % 2]
                ab, Rab = Ab[0][0:64], R_Ab[0]
                first = True
                for r in range(2):
                    for c in range(2):
                        for t in range(16):
                            K.dma("sp", lam(lambda e, r, c, t: e.dma_start(
                                out=zb[c * 64 + 4 * t:c * 64 + 4 * t + 4, r * 32:(r + 1) * 32, :],
                                in_=zout_v[r, c, cb, t // 4, t % 4]), r, c, t),
                                reads=[R_zout], writes=[Rzb], add=not first)
                            first = False
                def stA(b8):
                    bk = b8 % 4
                    K.group("pe", [lam(lambda e, j: e.matmul(bank(bk)[0:64, j * 64:(j + 1) * 64], zb[:, :, 8 * b8 + j], WA,
                                                              start=True, stop=True), j) for j in range(8)],
                            reads=[Rzb, R_k], writes=[pb[bk]])
                    dst = ab[:, 8 * b8:8 * b8 + 8, :]
                    srcp = bank(bk)[0:64, :].rearrange("p (j c) -> p j c", c=64)
                    if b8 % 2 == 0:
                        K.op("act", lambda e: e.activation(out=dst, in_=srcp, func=AF.Identity), reads=[pb[bk]], writes=[Rab])
                    else:
                        K.op("dve", lambda e: e.tensor_copy(out=dst, in_=srcp), reads=[pb[bk], Rab], writes=[Rab])
                for b8 in range(16):
                    stA(b8)

                def stB(k8):
                    bk = 4 + k8 % 4
                    fns = []
                    for j in range(8):
                        k1l = 8 * k8 + j
                        for c2 in range(2):
                            fns.append(lam(lambda e, j, k1l, c2: e.matmul(
                                bank(bk)[:, j * 64:(j + 1) * 64], ab[:, :, c2 * 32 + k1l], MB[0:64, k1l, c2, :],
                                start=(c2 == 0), stop=(c2 == 1)), j, k1l, c2))
                    K.group("pe", fns, reads=[Rab, R_k], writes=[pb[bk]])
                    dst = FT[:, cb, 512 * k8:512 * (k8 + 1)]
                    if k8 % 2 == 0:
                        K.op("act", lambda e: e.activation(out=dst, in_=bank(bk), func=AF.Identity), reads=[pb[bk]], writes=[R_FT[cb]])
                    else:
                        K.op("dve", lambda e: e.tensor_copy(out=dst, in_=bank(bk)), reads=[pb[bk], R_FT[cb]], writes=[R_FT[cb]])
                for k8 in range(4):
                    stB(k8)

            for cb in range(8):
                pos_dft(cb)
            cw = {"n": 0}
            xTv = xT.rearrange("p k (r c) -> p k c r", c=32)

            def fw_block(j, oc):
                bk = cw["n"] % 4
                cw["n"] += 1
                K.group("pe", [lam(lambda e, q: e.matmul(bank(bk), FWP[:, q, oc * 128:(oc + 1) * 128],
                                                          FT[:, q, 512 * j:512 * (j + 1)],
                                                          start=(q == 0), stop=(q == 7)), q) for q in range(8)],
                        reads=[R_k] + R_FT, writes=[pb[bk]])
                xv = xTv[:, oc, 8 * j:8 * j + 8, :]
                K.op("dve", lambda e: e.scalar_tensor_tensor(
                    out=xv, in0=bank(bk).rearrange("p (c r) -> p c r", c=8), scalar=sc[:, l, 6, oc:oc + 1], in1=xv,
                    op0=ALU.mult, op1=ALU.add),
                    reads=[pb[bk], R_sc] + [R_xT[oc][tb] for tb in range(4)], writes=[R_xT[oc][tb] for tb in range(4)])

            for j in range(4):
                for oc in range(8):
                    fw_block(j, oc)
            K.barrier()

        if mode in L1MODES:
            mlp(0, final_store=l1_front_blk, tail_setup=l1_front_setup)
            fourier_back()
        elif mode != "attn_only":
            mlp(0)
        A.off = OFF_R1
        ot = [A.alloc([128, D], F32) for _ in range(2)]
        R_ot = [Region("ot0"), Region("ot1")]

        def store_tile(t):
            b0 = 2 * (t % 2)
            pst = bank2(b0)
            K.group("pe", [lam(lambda e, k: e.transpose(pst[:, k * 128:(k + 1) * 128], xT[:, k, t * 128:(t + 1) * 128], ident_f), k)
                           for k in range(8)],
                    reads=[R_const] + [R_xT[k][t // 4] for k in range(8)], writes=[pb[b0], pb[b0 + 1]])
            o_, Ro = ot[t % 2], R_ot[t % 2]
            if t % 2 == 0:
                K.op("act", lambda e: e.activation(out=o_, in_=pst, func=AF.Identity), reads=[pb[b0], pb[b0 + 1]], writes=[Ro])
            else:
                K.op("dve", lambda e: e.tensor_copy(out=o_, in_=pst), reads=[pb[b0], pb[b0 + 1]], writes=[Ro])
            K.dma("sp", lambda e: e.dma_start(out=out[t * 128:(t + 1) * 128, :], in_=o_), reads=[Ro])

        def store_block(tb):
            for t in range(4 * tb, 4 * tb + 4):
                store_tile(t)
        if mode == "full":
            mlp(1, final_store=store_block)
        else:
            for tb in range(4):
                store_block(tb)

        final = [(K.dsem["sp"][i], v) for i, v in enumerate(K.dval["sp"]) if v]
        K.ops["sp"].append((final, None, None))
        K.emit()
    return nc


def _bias_tables(rpb, s):
    own_cols = np.arange(32) + 32 * s
    halo_cols = np.arange(32, 40) if s == 0 else np.arange(24, 32)
    tab = np.full((16, 10, 128, 128), NEG, np.float32)

    def fill(ci, qrows, krows, kcols):
        kr = np.repeat(krows, len(kcols))
        kc = np.tile(kcols, len(krows))
        qr = np.repeat(qrows, 32)
        qc = np.tile(own_cols, len(qrows))
        rs = np.clip(qr - 4, 0, 56)
        cs = np.clip(qc - 8, 0, 48)
        ok = (kr[:, None] >= rs[None, :]) & (kr[:, None] < rs[None, :] + 8) & \
             (kc[:, None] >= cs[None, :]) & (kc[:, None] < cs[None, :] + 16)
        dr = np.clip(kr[:, None] - qr[None, :] + 7, 0, 14)
        dc = np.clip(kc[:, None] - qc[None, :] + 15, 0, 30)
        vals = rpb[:, dr, dc]
        blk = np.where(ok[None], vals, np.float32(NEG)).astype(np.float32)
        tab[:, ci, :blk.shape[1], :] = blk

    def tile_chunks(i, c0):
        qrows = np.arange(4 * i, 4 * i + 4)
        own = [g for g in (i - 1, i, i + 1) if 0 <= g < 16]
        for j, g in enumerate(own):
            fill(c0 + j, qrows, np.arange(4 * g, 4 * g + 4), own_cols)
        r0, r1 = max(4 * i - 4, 0), min(4 * i + 8, 64)
        fill(c0 + len(own), qrows, np.arange(r0, r1), halo_cols)

    tile_chunks(0, 0)
    tile_chunks(5, 3)
    tile_chunks(15, 7)
    return np.ascontiguousarray(tab.transpose(0, 2, 1, 3).reshape(16, 128, 1280))


def _consts():
    c = np.zeros((128, 256), np.float32)
    c[:, 0:128] = np.eye(128, dtype=np.float32)
    bo = np.zeros((128, 128), np.float32)
    bo[:64, :64] = 1.0 / 64.0
    bo[64:, 64:] = 1.0 / 64.0
    c[:, 128:256] = bo
    return c


def _dft_consts(s):
    j = np.arange(256, dtype=np.float64)
    ang = 2 * np.pi * np.outer(j, j) / 256.0
    dftc = np.concatenate([np.cos(ang), np.sin(ang)], 1).astype(np.float32)
    n1 = np.arange(64, dtype=np.float64)
    k1 = 32 * s + np.arange(32, dtype=np.float64)
    th = 2 * np.pi * np.outer(n1, k1) / 64.0
    wsta = np.zeros((2, 64, 2, 32), np.float64)
    wsta[0, :, 0], wsta[0, :, 1] = np.cos(th), np.sin(th)
    wsta[1, :, 0], wsta[1, :, 1] = -np.sin(th), np.cos(th)
    n2 = np.arange(64, dtype=np.float64)
    k2 = np.arange(64, dtype=np.float64)
    ph = 2 * np.pi * (n2[:, None, None] * k2[None, None, :] / 64.0 + n2[:, None, None] * k1[None, :, None] / 4096.0)
    mb = np.zeros((64, 32, 2, 64), np.float64)
    mb[:, :, 0, :] = np.cos(ph) / 1024.0
    mb[:, :, 1, :] = -np.sin(ph) / 1024.0
    return dftc, wsta.reshape(128, 64).astype(np.float32), mb.reshape(64, 32 * 2 * 64).astype(np.float32)


_NC_CACHE = {}


def _get_nc(mode):
    if mode not in _NC_CACHE:
        _NC_CACHE[mode] = build_program(mode)
    return _NC_CACHE[mode]


def kernel(x, c, ctx, c_ctx, ada_w, ada_b, norm_g, attn_wqkv, attn_wo, q_norm_g, k_norm_g, rpb,
           fourier_w, mlp_w1, mlp_w2, _mode="full"):
    f = lambda a: np.ascontiguousarray(np.asarray(a, dtype=np.float32))
    x, c, ctx, c_ctx = f(x), f(c), f(ctx), f(c_ctx)
    ada_w, ada_b, norm_g = f(ada_w), f(ada_b), f(norm_g)
    wqkv, wo, fw = f(attn_wqkv)[0], f(attn_wo)[0], f(fourier_w)[0]
    gq, gk = f(q_norm_g)[0], f(k_norm_g)[0]
    rpb0 = f(rpb)[0]
    w1, w2 = f(mlp_w1), f(mlp_w2)
    cm = _consts()
    nc = _get_nc(_mode)
    in_maps = []
    xg = x.reshape(4, 64, 64, D)
    tabs = [_bias_tables(rpb0, s) for s in range(2)]
    dfts = [_dft_consts(s) for s in range(2)]
    for core in range(8):
        b, s = core // 2, core % 2
        own = xg[b][:, 32 * s:32 * s + 32, :].reshape(NT, D)
        hal = (xg[b][:, 32:40, :] if s == 0 else xg[b][:, 24:32, :]).reshape(512, D)
        sv = np.zeros((256, 128), np.float32)
        sv[0:96] = ada_b.reshape(96, 128)
        sv[128:136] = c[b].reshape(8, 128)
        sv[136:144] = c_ctx.reshape(8, 128)
        sv[144:176] = norm_g.reshape(32, 128)
        sv[176, 0:64] = gq
        sv[176, 64:128] = gq
        sv[177, 0:64] = gk
        sv[177, 64:128] = gk
        in_maps.append({
            "xown": np.ascontiguousarray(own), "xhalo": np.ascontiguousarray(hal), "ctxb": ctx[b],
            "smallv": sv, "ada_w": ada_w, "wqkv": wqkv, "wo": wo, "fw": fw, "w1": w1, "w2": w2,
            "btab": tabs[s], "cmat": cm, "dftc": dfts[s][0], "wsta": dfts[s][1], "mbc": dfts[s][2],
        })
    res = run_bass_kernel_spmd(nc, in_maps, core_ids=list(range(8)))
    outp = np.empty((4, 64, 64, D), np.float32)
    for core in range(8):
        b, s = core // 2, core % 2
        outp[b][:, 32 * s:32 * s + 32, :] = res.results[core]["out"].reshape(64, 32, D)
    return outp.reshape(4, 4096, D)
```

```python
import numpy as np
import concourse.bass as bass
import concourse.mybir as mybir
from concourse.bass_utils import run_bass_kernel_spmd

F32 = mybir.dt.float32
BF16 = mybir.dt.bfloat16
U8 = mybir.dt.uint8
AF = mybir.ActivationFunctionType
ALU = mybir.AluOpType

D = 1024
NT = 2048
EPS = 1e-6
NEG = -30000.0


class Region:
    __slots__ = ("name", "w", "r")

    def __init__(self, name):
        self.name = name
        self.w = []
        self.r = []


class Sched:
    COMPUTE = ("pe", "act", "dve", "pool")

    def __init__(self, nc, sems, ndma=12):
        self.nc = nc
        self.ops = {e: [] for e in ("pe", "act", "dve", "pool", "sp")}
        self.sem = {e: sems[e] for e in self.COMPUTE}
        self.cnt = {e: 0 for e in self.COMPUTE}
        self.seen = {e: {} for e in self.ops}
        self.dsem = {"sp": sems["dsp"], "pool": sems["dpool"]}
        self.dctr = {"sp": 0, "pool": 0}
        self.dval = {"sp": [0] * len(sems["dsp"]), "pool": [0] * len(sems["dpool"])}
        self.extra = []

    def _waits(self, eng, toks):
        out = []
        seen = self.seen[eng]
        for (t, kind) in toks:
            key, val, src = t
            if src == eng and eng == "pe":
                continue
            if seen.get(key, 0) >= val:
                continue
            seen[key] = val
            out.append((key, val))
        return out

    def _deps(self, reads, writes):
        toks = []
        for r in reads:
            toks.extend((t, "raw") for t in r.w)
        for w in writes:
            toks.extend((t, "waw") for t in w.w)
            toks.extend((t, "war") for t in w.r)
        return toks

    def op(self, eng, fn, reads=(), writes=()):
        waits = self._waits(eng, self._deps(reads, writes))
        self.cnt[eng] += 1
        tok = (self.sem[eng], self.cnt[eng], eng)
        self.ops[eng].append((waits, fn, (self.sem[eng], 1)))
        for r in reads:
            r.r.append(tok)
        for w in writes:
            w.w = [tok]
            w.r = []
        return tok

    def group(self, eng, fns, reads=(), writes=()):
        waits = self._waits(eng, self._deps(reads, writes))
        self.cnt[eng] += 1
        tok = (self.sem[eng], self.cnt[eng], eng)
        for i, fn in enumerate(fns):
            self.ops[eng].append((waits if i == 0 else [], fn,
                                  (self.sem[eng], 1) if i == len(fns) - 1 else None))
        for r in reads:
            r.r.append(tok)
        for w in writes:
            w.w = [tok]
            w.r = []
        return tok

    def dma(self, q, fn, reads=(), writes=(), add=False):
        pool = self.dsem[q]
        i = self.dctr[q] % len(pool)
        self.dctr[q] += 1
        sem = pool[i]
        toks = self._deps(reads, writes)
        if add:
            toks = [(t, k) for (t, k) in toks if k != "waw"]
        prev = self.dval[q][i]
        if prev:
            toks.append(((sem, prev, "dma"), "raw"))
        waits = self._waits(q, toks)
        self.dval[q][i] = prev + 16
        tok = (sem, prev + 16, "dma")
        self.ops[q].append((waits, fn, (sem, 16)))
        for r in reads:
            r.r.append(tok)
        for w in writes:
            if add:
                w.w = w.w + [tok]
            else:
                w.w = [tok]
                w.r = []
        return tok

    def special(self, q, fn, sem, val, reads=(), writes=(), add=False):
        waits = self._waits(q, self._deps(reads, writes))
        tok = (sem, val, "dma")
        self.ops[q].append((waits, fn, (sem, 1)))
        self.extra.append(tok)
        for r in reads:
            r.r.append(tok)
        for w in writes:
            if add:
                w.w = w.w + [tok]
            else:
                w.w = [tok]
                w.r = []
        return tok

    def barrier(self):
        toks = [((self.sem[e], self.cnt[e], e), "raw") for e in self.COMPUTE if self.cnt[e]]
        for q in ("sp", "pool"):
            toks += [((self.dsem[q][i], v, "dma"), "raw") for i, v in enumerate(self.dval[q]) if v]
        toks += [(t, "raw") for t in self.extra]
        for eng in self.ops:
            self.ops[eng].append((self._waits(eng, toks), None, None))

    def emit(self):
        nc = self.nc
        with nc.Block() as block:
            def run(lst):
                def f(e):
                    for waits, fn, inc in lst:
                        for (s, v) in waits:
                            e.wait_ge(s, v)
                        if fn is None:
                            continue
                        ins = fn(e)
                        if inc is not None:
                            ins.then_inc(inc[0], inc[1])
                return f
            block.tensor(run(self.ops["pe"]))
            block.scalar(run(self.ops["act"]))
            block.vector(run(self.ops["dve"]))
            block.gpsimd(run(self.ops["pool"]))
            block.sync(run(self.ops["sp"]))


class Arena:
    def __init__(self, t, size):
        self.t = t
        self.size = size
        self.off = 0
        self.marks = []

    def alloc(self, shape, dt):
        esz = 2 if dt == BF16 else 4
        n = 1
        for s in shape[1:]:
            n *= s
        nbytes = (n * esz + 63) // 64 * 64
        assert self.off + nbytes <= self.size, f"arena overflow {self.off}+{nbytes}>{self.size}"
        ap = self.t[0:shape[0], self.off:self.off + n * esz].bitcast(dt)
        self.off += nbytes
        if len(shape) > 2:
            names = " ".join(f"d{i}" for i in range(1, len(shape)))
            kw = {f"d{i}": shape[i] for i in range(1, len(shape))}
            ap = ap.rearrange(f"p ({names}) -> p {names}", **kw)
        return ap

    def push(self):
        self.marks.append(self.off)

    def pop(self):
        self.off = self.marks.pop()


ARENA_BYTES = 188 * 1024


def build_program(mode="full"):
    nc = bass.Bass("TRN2", target_bir_lowering=False)

    def din(name, shape, dt=F32):
        return nc.dram_tensor(name, list(shape), dt, kind="ExternalInput").ap()

    xown = din("xown", [NT, D])
    xhalo = din("xhalo", [512, D])
    ctxb = din("ctxb", [256, D])
    smallv = din("smallv", [256, 128])
    ada_w = din("ada_w", [2, D, 6 * D])
    wqkv = din("wqkv", [D, 3 * D])
    wo = din("wo", [D, D])
    fw = din("fw", [D, D])
    w1 = din("w1", [2, D, 4 * D])
    w2 = din("w2", [2, 4 * D, D])
    btab = din("btab", [16, 128, 10 * 128])
    cmat = din("cmat", [128, 256])
    dftc = din("dftc", [256, 512])
    wsta = din("wsta", [128, 64])
    mbc = din("mbc", [64, 32 * 2 * 64])
    zin = nc.dram_tensor("zin", [2 * 64 * 8, 32 * 128], BF16)
    zout = nc.dram_tensor("zout", [2 * 2 * 64 * 8, 32 * 128], BF16)
    out = nc.dram_tensor("out", [NT, D], F32, kind="ExternalOutput").ap()

    import contextlib
    with contextlib.ExitStack() as es:
        arena_t = es.enter_context(nc.sbuf_tensor("arena", [128, ARENA_BYTES], U8))
        psum_t = es.enter_context(nc.psum_tensor("psum", [128, 8, 512], F32))
        sems = {}
        for e in ("pe", "act", "dve", "pool"):
            sems[e] = es.enter_context(nc.semaphore("s_" + e))
        sems["cc"] = [es.enter_context(nc.semaphore(f"s_cc{i}")) for i in range(16)]
        sems["dsp"] = [es.enter_context(nc.semaphore(f"dsp{i}")) for i in range(16)]
        sems["dpool"] = [es.enter_context(nc.semaphore(f"dpl{i}")) for i in range(16)]
        K = Sched(nc, sems)
        A = Arena(arena_t, ARENA_BYTES)

        pb = [Region(f"bank{i}") for i in range(8)]

        def bank(i):
            return psum_t[:, i, :]

        def bank2(i):
            return psum_t[:, i:i + 2, :].rearrange("p a c -> p (a c)")

        def bank_bf(i):
            return psum_t[:, i, :].bitcast(BF16)

        def lam(f, *a):
            return lambda e: f(e, *a)

        K.op("dve", lambda e: e.memset(psum_t[:, :, :], 0.0), writes=pb)
        ident_f = A.alloc([128, 128], F32)
        ident_b = A.alloc([128, 128], BF16)
        bones = A.alloc([128, 128], BF16)
        ones_b = A.alloc([128, 128], BF16)
        epsv = A.alloc([128, 2], F32)
        R_const = Region("const")
        R_c2 = Region("const2")
        K.dma("sp", lambda e: e.dma_start(out=ident_f, in_=cmat[:, 0:128]), writes=[R_const])
        K.dma("pool", lambda e: e.dma_start(out=ident_b, in_=cmat[:, 0:128]), writes=[R_const], add=True)
        K.dma("pool", lambda e: e.dma_start(out=bones, in_=cmat[:, 128:256]), writes=[R_const], add=True)
        K.op("pool", lambda e: e.memset(ones_b, 1.0 / 1024.0), writes=[R_c2])
        K.op("pool", lambda e: e.memset(epsv[:, 0:1], EPS), writes=[R_c2])
        K.op("pool", lambda e: e.memset(epsv[:, 1:2], 1024.0 * EPS), writes=[R_c2])

        TA = A.alloc([128, 128], F32)
        TB = A.alloc([128, 128], F32)
        sc = A.alloc([128, 2, 9, 8], F32)
        mods = A.alloc([128, 2, 6, 8, 2], F32)
        R_TA, R_TB, R_sc, R_mods = Region("TA"), Region("TB"), Region("sc"), Region("mods")
        cTs = A.alloc([128, 8, 2], BF16)
        R_cTs = Region("cTs")
        qg = A.alloc([128, 2], F32)
        R_qg = Region("qg")
        BASE = A.off
        OFF_R1, OFF_U, OFF_R3 = BASE, BASE + 32 * 1024, BASE + 96 * 1024
        A.off = OFF_U
        stA = A.alloc([128, 128], F32)
        stB = A.alloc([128, 128], F32)
        R_stA, R_stB = Region("stA"), Region("stB")
        K.dma("sp", lambda e: e.dma_start(out=stA, in_=smallv[0:128, :]), writes=[R_stA])
        K.dma("sp", lambda e: e.dma_start(out=stB, in_=smallv[128:256, :]), writes=[R_stB])
        K.op("pe", lambda e: e.transpose(bank(6)[:, 0:128], stA, ident_f), reads=[R_stA, R_const], writes=[pb[6]])
        K.op("dve", lambda e: e.tensor_copy(out=TA, in_=bank(6)[:, 0:128]), reads=[pb[6]], writes=[R_TA])
        K.op("pe", lambda e: e.transpose(bank(6)[:, 128:256], stB, ident_f), reads=[R_stB, R_const], writes=[pb[6]])
        K.op("dve", lambda e: e.tensor_copy(out=TB, in_=bank(6)[:, 128:256]), reads=[pb[6]], writes=[R_TB])
        gn = TB[:, 16:48].rearrange("p (l i k) -> p l i k", l=2, i=2)
        qkg_t = TB[:, 48:50]
        abT = TA[:, 0:96].rearrange("p (l vk) -> p l vk", l=2)
        K.op("act", lambda e: e.activation(out=cTs.rearrange("p k j -> p j k"),
                                           in_=TB[:, 0:16].rearrange("p (j k) -> p j k", j=2), func=AF.Silu),
             reads=[R_TB], writes=[R_cTs])
        K.op("dve", lambda e: e.scalar_tensor_tensor(out=qg[:, 0:1], in0=qkg_t[:, 0:1], scalar=0.125, in1=qkg_t[:, 1:2],
                                                     op0=ALU.mult, op1=ALU.mult), reads=[R_TB], writes=[R_qg])

        ADA_BLK = 768
        nblk = 6 * D // ADA_BLK
        A.off = OFF_R3
        awb = [A.alloc([128, 8, ADA_BLK], BF16) for _ in range(2)]
        R_awb = [Region("awb0"), Region("awb1")]
        OFF_R3_ATT = A.off
        modsv = mods.rearrange("p l v k j -> p l (v k) j")
        ada_ctr = {"n": 0}

        def ada_dma(l, blk):
            bi = ada_ctr["n"] % 2
            ada_ctr["n"] += 1
            src_ = ada_w[l].rearrange("(k p) f -> p k f", p=128)[:, :, blk * ADA_BLK:(blk + 1) * ADA_BLK]
            K.dma("pool", lambda e: e.dma_start(out=awb[bi], in_=src_), writes=[R_awb[bi]])
            return bi

        def ada_pe(l, blk, bi):
            nf = ADA_BLK // 128
            fns = []
            for fc in range(nf):
                for k in range(8):
                    fns.append(lam(lambda e, fc, k: e.matmul(
                        bank(7)[:, 2 * fc:2 * fc + 2], awb[bi][:, k, fc * 128:(fc + 1) * 128], cTs[:, k, :],
                        start=(k == 0), stop=(k == 7)), fc, k))
            K.group("pe", fns, reads=[R_awb[bi], R_cTs], writes=[pb[7]])
            for j in range(2):
                K.op("dve", lam(lambda e, j: e.tensor_tensor(
                    out=modsv[:, l, nf * blk:nf * blk + nf, j],
                    in0=bank(7)[:, 0:2 * nf].rearrange("p (f j) -> p f j", j=2)[:, :, j],
                    in1=abT[:, l, nf * blk:nf * blk + nf], op=ALU.add), j), reads=[pb[7], R_TA], writes=[R_mods])

        def sc_mul(l, i, v, j, ni, mul):
            K.op("dve", lambda e: e.scalar_tensor_tensor(
                out=sc[:, l, i, :], in0=mods[:, l, v, :, j], scalar=1.0, in1=gn[:, l, ni, :],
                op0=ALU.add, op1=ALU.mult), reads=[R_mods, R_TB], writes=[R_sc])
            if mul != 1.0:
                K.op("dve", lambda e: e.tensor_scalar(out=sc[:, l, i, :], in0=sc[:, l, i, :], scalar1=mul,
                                                      scalar2=None, op0=ALU.mult), reads=[R_sc], writes=[R_sc])

        def sc_cp(l, i, v, j):
            K.op("dve", lambda e: e.tensor_copy(out=sc[:, l, i, :], in_=mods[:, l, v, :, j]),
                 reads=[R_mods], writes=[R_sc])

        def sc_first(l):
            sc_mul(l, 0, 1, 0, 0, 32.0)
            sc_mul(l, 2, 1, 1, 0, 32.0)
            sc_cp(l, 1, 0, 0)
            sc_cp(l, 3, 0, 1)

        def sc_rest(l):
            sc_mul(l, 4, 4, 0, 1, 1.0)
            sc_mul(l, 8, 1, 0, 0, 1.0)
            for (i, v, j) in ((5, 3, 0), (6, 2, 0), (7, 5, 0)):
                sc_cp(l, i, v, j)

        for blk in range(3):
            ada_pe(0, blk, ada_dma(0, blk))
        sc_first(0)
        ada_todo = [(0, blk) for blk in range(3, nblk)] + [(1, blk) for blk in range(nblk)]

        A.off = OFF_U
        xT = A.alloc([128, 8, NT], F32)
        R_xT = [[Region(f"xT{k}_{tb}") for tb in range(4)] for k in range(8)]

        class WStream:
            def __init__(self, nbuf, shape, name):
                self.bufs = [A.alloc(shape, BF16) for _ in range(nbuf)]
                self.regs = [Region(f"{name}{i}") for i in range(nbuf)]
                self.n = 0

            def load(self, pairs_fn):
                i = self.n % len(self.bufs)
                self.n += 1
                buf, reg = self.bufs[i], self.regs[i]
                for j, (dst, src) in enumerate(pairs_fn(buf)):
                    K.dma("pool", lam(lambda e, dst, src: e.dma_start(out=dst, in_=src), dst, src),
                          writes=[reg], add=(j > 0))
                return buf, reg

        L = 0
        A.off = OFF_R1
        OT = A.alloc([128, 8, NT], BF16)
        R_OT = [[Region(f"OT{h}_{i}") for i in range(16)] for h in range(8)]
        A.off = OFF_U
        NTOK0 = NT + 512 + 256
        hT = A.alloc([128, 8, NTOK0], BF16)
        R_hT8 = [[Region(f"hT{t}_{k}") for k in range(8)] for t in range(22)]
        R_hT = None

        xin = [A.alloc([128, D], F32) for _ in range(3)]
        R_xin = [Region(f"xin{i}") for i in range(3)]
        xn = [A.alloc([128, D], BF16) for _ in range(2)]
        R_xn = [Region(f"xn{i}") for i in range(2)]
        junk = A.alloc([128, D], BF16)
        R_junk = Region("junk")
        ss = A.alloc([128, 8], F32)
        R_ss = [Region(f"ss{i}") for i in range(4)]

        def tile_src(t):
            if t < 16:
                return xown[t * 128:(t + 1) * 128, :]
            if t < 20:
                return xhalo[(t - 16) * 128:(t - 15) * 128, :]
            return ctxb[(t - 20) * 128:(t - 19) * 128, :]

        def phase0a_stats(t):
            xi, Rxi = xin[t % 3], R_xin[t % 3]
            s1 = ss[:, (t % 4) * 2:(t % 4) * 2 + 1]
            Rs = R_ss[t % 4]
            K.dma("sp", lambda e: e.dma_start(out=xi, in_=tile_src(t)), writes=[Rxi])
            K.op("act", lambda e: e.activation(out=junk, in_=xi, func=AF.Square, accum_out=s1),
                 reads=[Rxi], writes=[R_junk, Rs])

        def phase0a_tile(t):
            xi, Rxi = xin[t % 3], R_xin[t % 3]
            xo, Rxo = xn[t % 2], R_xn[t % 2]
            s1 = ss[:, (t % 4) * 2:(t % 4) * 2 + 1]
            s2 = ss[:, (t % 4) * 2 + 1:(t % 4) * 2 + 2]
            Rs = R_ss[t % 4]
            K.op("act", lambda e: e.activation(out=s2, in_=s1, func=AF.Ln, bias=epsv[:, 1:2]), reads=[Rs, R_c2], writes=[Rs])
            K.op("act", lambda e: e.activation(out=s1, in_=s2, func=AF.Exp, scale=-0.5), reads=[Rs], writes=[Rs])
            K.op("act", lambda e: e.activation(out=xo, in_=xi, func=AF.Identity, scale=s1),
                 reads=[Rxi, Rs], writes=[Rxo])
            bk = 6 + (t % 2)
            pbf = bank_bf(bk)
            K.group("pe", [lam(lambda e, k: e.transpose(pbf[:, k * 128:(k + 1) * 128], xo[:, k * 128:(k + 1) * 128], ident_b), k)
                           for k in range(8)], reads=[Rxo, R_const], writes=[pb[bk]])
            ia, ib = (0, 1) if t < 20 else (2, 3)
            for k in range(8):
                K.op("dve", lam(lambda e, k: e.tensor_scalar(
                    out=hT[:, k, t * 128:(t + 1) * 128], in0=pbf[:, k * 128:(k + 1) * 128],
                    scalar1=sc[:, L, ia, k:k + 1], scalar2=sc[:, L, ib, k:k + 1], op0=ALU.mult, op1=ALU.add), k),
                    reads=[pb[bk], R_sc], writes=[R_hT8[t][k]])

        phase0a_stats(0)
        for t in range(22):
            if t + 1 < 22:
                phase0a_stats(t + 1)
            phase0a_tile(t)
        assert A.off <= OFF_R3, A.off
        A.off = OFF_R3_ATT

        WQ = WStream(2, [128, 8, 384], "wqkv")
        QT = A.alloc([128, NT], BF16)
        KT = A.alloc([128, NTOK0], BF16)
        Vown = A.alloc([128, 16, 256], BF16)
        Vctx = A.alloc([128, 2, 256], BF16)
        Vhal = A.alloc([128, 16, 256], BF16)
        R_QT = [Region(f"QT{i}") for i in range(4)]
        R_KT = [Region(f"KT{i}") for i in range(6)]
        R_Vown = [Region(f"Vown{i}") for i in range(4)]
        R_Vctx = Region("Vctx")
        R_Vhal = [Region(f"Vhal{i}") for i in range(4)]
        R_ones = Region("vones")
        for V in (Vown, Vctx, Vhal):
            K.op("pool", lam(lambda e, V: e.memset(V[:, :, 64:192], 1.0), V), writes=[R_ones])
        sq = [A.alloc([128, 512], BF16) for _ in range(2)]
        R_sq = [Region("sq0"), Region("sq1")]
        tt = [A.alloc([128, 512], F32) for _ in range(2)]
        R_tt = [Region("tt0"), Region("tt1")]
        BT = [A.alloc([128, 10 * 128], BF16) for _ in range(2)]
        R_BT = [Region("BT0"), Region("BT1")]
        PT = [A.alloc([128, 768], BF16) for _ in range(4)]
        R_PT = [Region(f"PT{i}") for i in range(4)]
        rden = [A.alloc([128, 128], F32) for _ in range(2)]
        R_rden = [Region("rden0"), Region("rden1")]
        ctr = {"qk": 0, "pt": 0, "v": 0}

        def wq_src(hp):
            def f(buf):
                wv = wqkv.rearrange("(k p) f -> p k f", p=128)
                return [(buf[:, :, j * 128:(j + 1) * 128], wv[:, :, j * 1024 + hp * 128: j * 1024 + hp * 128 + 128])
                        for j in range(3)]
            return f

        def load_bt(hp, e2):
            K.dma("pool", lambda e: e.dma_start(out=BT[e2], in_=btab[2 * hp + e2]), writes=[R_BT[e2]])
            K.op("act", lambda e: e.activation(out=BT[e2], in_=BT[e2], func=AF.Exp), reads=[R_BT[e2]], writes=[R_BT[e2]])

        def proj_norm(wbuf, Rw, wcol, tok0, ntok, dst, Rdst, is_q):
            n = ctr["qk"]
            ctr["qk"] += 1
            pa, pn = 6 + (n % 2), 4 + (n % 2)
            sqb, Rsq = sq[n % 2], R_sq[n % 2]
            ttb, Rtt = tt[n % 2], R_tt[n % 2]
            tiles = sorted(set(range(tok0 // 128, (tok0 + ntok + 127) // 128)))
            K.group("pe", [lam(lambda e, k: e.matmul(bank(pa)[:, 0:ntok], wbuf[:, k, wcol * 128:(wcol + 1) * 128],
                                                      hT[:, k, tok0:tok0 + ntok], start=(k == 0), stop=(k == 7)), k)
                           for k in range(8)], reads=[Rw] + [R_hT8[t][k] for t in tiles for k in range(8)], writes=[pb[pa]])
            K.op("act", lambda e: e.activation(out=sqb[:, 0:ntok], in_=bank(pa)[:, 0:ntok], func=AF.Square),
                 reads=[pb[pa]], writes=[Rsq])
            K.op("pe", lambda e: e.matmul(bank(pn)[:, 0:ntok], bones, sqb[:, 0:ntok], start=True, stop=True),
                 reads=[Rsq, R_const], writes=[pb[pn]])
            K.op("act", lambda e: e.activation(out=ttb[:, 0:ntok], in_=bank(pn)[:, 0:ntok], func=AF.Ln, bias=epsv[:, 0:1]),
                 reads=[pb[pn], R_c2], writes=[Rtt])
            K.op("act", lambda e: e.activation(out=ttb[:, 0:ntok], in_=ttb[:, 0:ntok], func=AF.Exp, scale=-0.5),
                 reads=[Rtt], writes=[Rtt])
            scal = qg[:, 0:1] if is_q else 1.0
            K.op("dve", lambda e: e.scalar_tensor_tensor(out=dst, in0=bank(pa)[:, 0:ntok], scalar=scal,
                                                         in1=ttb[:, 0:ntok], op0=ALU.mult, op1=ALU.mult),
                 reads=[pb[pa], Rtt, R_qg], writes=[Rdst])

        VT = A.alloc([128, NTOK0], BF16)
        R_VT = [Region(f"VT{i}") for i in range(6)]

        def vT_proj(wbuf, Rw, tok0, ntok, Rdst):
            n = ctr["qk"]
            ctr["qk"] += 1
            pa = 6 + (n % 2)
            tiles = sorted(set(range(tok0 // 128, (tok0 + ntok + 127) // 128)))
            K.group("pe", [lam(lambda e, k: e.matmul(bank(pa)[:, 0:ntok], wbuf[:, k, 256:384],
                                                      hT[:, k, tok0:tok0 + ntok], start=(k == 0), stop=(k == 7)), k)
                           for k in range(8)], reads=[Rw] + [R_hT8[t][k] for t in tiles for k in range(8)], writes=[pb[pa]])
            if n % 2 == 0:
                K.op("act", lambda e: e.activation(out=VT[:, tok0:tok0 + ntok], in_=bank(pa)[:, 0:ntok], func=AF.Identity),
                     reads=[pb[pa]], writes=[Rdst])
            else:
                K.op("dve", lambda e: e.tensor_copy(out=VT[:, tok0:tok0 + ntok], in_=bank(pa)[:, 0:ntok]),
                     reads=[pb[pa]], writes=[Rdst])

        def vproj(tok_list, dstV, s0, Rdst):
            bk = 4 + ctr["v"] % 2
            ctr["v"] += 1
            pbf = bank_bf(bk)
            fns = []
            rd = set()
            for j, (t0, ln) in enumerate(tok_list):
                fns.append(lam(lambda e, j, t0, ln: e.transpose(pbf[0:ln, j * 128:(j + 1) * 128], VT[:, t0:t0 + ln], ident_b),
                               j, t0, ln))
                rd.add(t0 // 512 if t0 < 2560 else 5)
                rd.add((t0 + ln - 1) // 512 if t0 + ln - 1 < 2560 else 5)
            K.group("pe", fns, reads=[R_const] + [R_VT[r] for r in sorted(rd)], writes=[pb[bk]])
            n = len(tok_list)
            lnmax = max(ln for _, ln in tok_list)
            src_ = pbf[0:lnmax, 0:n * 128].rearrange("p (j c) -> p j c", c=128)
            K.op("act", lambda e: e.activation(out=dstV[0:lnmax, s0:s0 + n, 0:64], in_=src_[:, :, 0:64], func=AF.Identity),
                 reads=[pb[bk]], writes=[Rdst])
            K.op("dve", lambda e: e.tensor_copy(out=dstV[0:lnmax, s0:s0 + n, 192:256], in_=src_[:, :, 64:128]),
                 reads=[pb[bk], Rdst], writes=[Rdst])

        def halo_rng(i):
            r0 = max(4 * i - 4, 0)
            r1 = min(4 * i + 8, 64)
            return 2048 + r0 * 8, (r1 - r0) * 8

        def attn_A(hp, e2, i):
            p0 = 64 * e2
            own = [g for g in (i - 1, i, i + 1) if 0 <= g < 16]
            no = len(own)
            h0, hl = halo_rng(i)
            tb0 = 0 if i == 0 else (7 if i == 15 else 3)
            n = ctr["pt"]
            ctr["pt"] += 1
            sb = (0, 2, 6)[n % 3]
            ptb, Rpt = PT[n % 4], R_PT[n % 4]
            sps = bank2(sb)
            qs = QT[p0:p0 + 64, i * 128:(i + 1) * 128]
            chunks = [(g * 128, 128, "own", g) for g in own] + [(h0, hl, "hal", i)] + \
                     [(2560, 128, "ctx", 0), (2688, 128, "ctx", 1)]
            fns = []
            kr = set()
            for j, (t0, ln, _, _) in enumerate(chunks):
                fns.append(lam(lambda e, j, t0, ln: e.matmul(sps[0:ln, j * 128:(j + 1) * 128], KT[p0:p0 + 64, t0:t0 + ln],
                                                              qs, start=True, stop=True), j, t0, ln))
                kr.add(t0 // 512 if t0 < 2560 else 5)
            K.group("pe", fns, reads=[R_QT[i // 4]] + [R_KT[r] for r in sorted(kr)], writes=[pb[sb], pb[sb + 1]])
            ncol = (no + 3) * 128
            K.op("act", lambda e: e.activation(out=ptb[:, 0:ncol], in_=sps[:, 0:ncol], func=AF.Exp),
                 reads=[pb[sb], pb[sb + 1]], writes=[Rpt])
            nb = (no + 1) * 128
            K.op("dve", lambda e: e.tensor_tensor(out=ptb[:, 0:nb], in0=ptb[:, 0:nb],
                                                  in1=BT[e2][:, tb0 * 128:tb0 * 128 + nb], op=ALU.mult),
                 reads=[Rpt, R_BT[e2]], writes=[Rpt])
            return (hp, e2, i, n, chunks, ptb, Rpt)

        def attn_B(st):
            hp, e2, i, n, chunks, ptb, Rpt = st
            p0 = 64 * e2
            ob = 4 + (n % 2)
            fns = []
            vr = []
            for j, (t0, ln, kind, idx) in enumerate(chunks):
                if kind == "own":
                    vt, rr = Vown[:, idx, :], R_Vown[idx // 4]
                elif kind == "hal":
                    vt, rr = Vhal[:, idx, :], R_Vhal[idx // 4]
                else:
                    vt, rr = Vctx[:, idx, :], R_Vctx
                vr.append(rr)
                fns.append(lam(lambda e, j, ln, vt: e.matmul(
                    bank(ob)[:, 0:128], vt[0:ln, 128 * e2:128 * e2 + 128], ptb[0:ln, j * 128:(j + 1) * 128],
                    start=(j == 0), stop=(j == len(chunks) - 1)), j, ln, vt))
            K.group("pe", fns, reads=[Rpt, R_ones] + vr, writes=[pb[ob]])
            rd, Rrd = rden[n % 2], R_rden[n % 2]
            dn = 64 - p0
            K.op("act", lambda e: e.activation(out=rd[dn:dn + 64, :], in_=bank(ob)[dn:dn + 64, 0:128], func=AF.Ln),
                 reads=[pb[ob]], writes=[Rrd])
            K.op("act", lambda e: e.activation(out=rd[dn:dn + 64, :], in_=rd[dn:dn + 64, :], func=AF.Exp, scale=-1.0),
                 reads=[Rrd], writes=[Rrd])
            K.op("dve", lambda e: e.tensor_copy(out=rd[p0:p0 + 64, :], in_=rd[dn:dn + 64, :]), reads=[Rrd], writes=[Rrd])
            K.op("dve", lambda e: e.tensor_tensor(out=OT[p0:p0 + 64, hp, i * 128:(i + 1) * 128],
                                                  in0=bank(ob)[p0:p0 + 64, 0:128], in1=rd[p0:p0 + 64, :], op=ALU.mult),
                 reads=[pb[ob], Rrd], writes=[R_OT[hp][i]])

        nxt = WQ.load(wq_src(0))
        NHP = 8
        for hp in range(NHP):
            wbuf, Rw = nxt
            if hp + 1 < NHP:
                nxt = WQ.load(wq_src(hp + 1))
            for e2 in range(2):
                load_bt(hp, e2)
            ada_now = [(lb, ada_dma(*lb)) for lb in ada_todo[2 * hp:2 * hp + 2]]
            for tb in range(4):
                proj_norm(wbuf, Rw, 0, tb * 512, 512, QT[:, tb * 512:(tb + 1) * 512], R_QT[tb], True)
            for tb in range(4):
                proj_norm(wbuf, Rw, 1, tb * 512, 512, KT[:, tb * 512:(tb + 1) * 512], R_KT[tb], False)
            proj_norm(wbuf, Rw, 1, 2048, 512, KT[:, 2048:2560], R_KT[4], False)
            proj_norm(wbuf, Rw, 1, 2560, 256, KT[:, 2560:2816], R_KT[5], False)
            for tb in range(4):
                vT_proj(wbuf, Rw, tb * 512, 512, R_VT[tb])
            vT_proj(wbuf, Rw, 2048, 512, R_VT[4])
            vT_proj(wbuf, Rw, 2560, 256, R_VT[5])
            for g4 in range(4):
                vproj([(128 * (4 * g4 + j), 128) for j in range(4)], Vown, 4 * g4, R_Vown[g4])
            vproj([(2560, 128), (2688, 128)], Vctx, 0, R_Vctx)
            for g4 in range(4):
                vproj([halo_rng(4 * g4 + j) for j in range(4)], Vhal, 4 * g4, R_Vhal[g4])
            pend = []
            for e2 in range(2):
                for i in range(16):
                    pend.append(attn_A(hp, e2, i))
                    if len(pend) > 2:
                        attn_B(pend.pop(0))
            while pend:
                attn_B(pend.pop(0))
            for (lb, bi) in ada_now:
                ada_pe(lb[0], lb[1], bi)
        assert 2 * NHP >= len(ada_todo)
        sc_rest(0)
        sc_first(1)
        sc_rest(1)
        K.barrier()
        A.off = OFF_R3

        xin2 = [A.alloc([128, D], F32) for _ in range(3)]
        R_xin2 = [Region(f"xinb{i}") for i in range(3)]
        WO = WStream(1, [128, 8, D], "wo")
        wob, Rwo = WO.load(lambda buf: [(buf, wo.rearrange("(k p) f -> p k f", p=128))])

        def load_xT(t):
            xi, Rxi = xin2[t % 3], R_xin2[t % 3]
            K.dma("sp", lambda e: e.dma_start(out=xi, in_=xown[t * 128:(t + 1) * 128, :]), writes=[Rxi])
            b0 = 4 + 2 * (t % 2)
            pst = bank2(b0)
            K.group("pe", [lam(lambda e, k: e.transpose(pst[:, k * 128:(k + 1) * 128], xi[:, k * 128:(k + 1) * 128], ident_f), k)
                           for k in range(8)], reads=[Rxi, R_const], writes=[pb[b0], pb[b0 + 1]])
            K.op("act", lambda e: e.activation(out=xT[:, :, t * 128:(t + 1) * 128],
                                               in_=pst.rearrange("p (k c) -> p k c", c=128), func=AF.Identity),
                 reads=[pb[b0], pb[b0 + 1]], writes=[R_xT[k][t // 4] for k in range(8)])

        for t in range(16):
            load_xT(t)
        cw = {"n": 0}

        def wo_block(tb, oc):
            bk = cw["n"] % 4
            cw["n"] += 1
            K.group("pe", [lam(lambda e, hp: e.matmul(bank(bk), wob[:, hp, oc * 128:(oc + 1) * 128],
                                                       OT[:, hp, tb * 512:(tb + 1) * 512], start=(hp == 0), stop=(hp == 7)), hp)
                           for hp in range(8)],
                    reads=[Rwo] + [R_OT[hp][i] for hp in range(8) for i in range(4 * tb, 4 * tb + 4)], writes=[pb[bk]])
            K.op("dve", lambda e: e.scalar_tensor_tensor(
                out=xT[:, oc, tb * 512:(tb + 1) * 512], in0=bank(bk), scalar=sc[:, L, 6, oc:oc + 1],
                in1=xT[:, oc, tb * 512:(tb + 1) * 512], op0=ALU.mult, op1=ALU.add),
                reads=[pb[bk], R_sc, R_xT[oc][tb]], writes=[R_xT[oc][tb]])

        for tb in range(4):
            for oc in range(8):
                wo_block(tb, oc)
        K.barrier()

        def norm_mod(l, ia, ib, dst, R_dst, after_blk=None, bufs=None, run=True):
            if bufs is None:
                bufs = ([A.alloc([128, 512], BF16) for _ in range(2)], [Region("msq0"), Region("msq1")],
                        [A.alloc([128, 512], F32) for _ in range(2)], [Region("mtt0"), Region("mtt1")],
                        [A.alloc([128, 512], F32) for _ in range(2)], [Region("mtmp0"), Region("mtmp1")])
            sqs, R_sqs, tts, R_tts, tmp, R_tmp = bufs

            def blk(tb):
                tsl = slice(tb * 512, (tb + 1) * 512)
                pn = 4 + tb % 2
                for k in range(8):
                    K.op("act", lam(lambda e, k: e.activation(out=sqs[k % 2], in_=xT[:, k, tsl], func=AF.Square), k),
                         reads=[R_xT[k][tb]], writes=[R_sqs[k % 2]])
                    K.op("pe", lam(lambda e, k: e.matmul(bank(pn), ones_b, sqs[k % 2], start=(k == 0), stop=(k == 7)), k),
                         reads=[R_sqs[k % 2], R_c2], writes=[pb[pn]])
                ttb, Rtt = tts[tb % 2], R_tts[tb % 2]
                K.op("act", lambda e: e.activation(out=ttb, in_=bank(pn), func=AF.Ln, bias=epsv[:, 0:1]),
                     reads=[pb[pn], R_c2], writes=[Rtt])
                K.op("act", lambda e: e.activation(out=ttb, in_=ttb, func=AF.Exp, scale=-0.5), reads=[Rtt], writes=[Rtt])
                for k in range(8):
                    tm, Rtm = tmp[k % 2], R_tmp[k % 2]
                    K.op("dve", lam(lambda e, k, tm: e.tensor_tensor(out=tm, in0=xT[:, k, tsl], in1=ttb, op=ALU.mult), k, tm),
                         reads=[R_xT[k][tb], Rtt], writes=[Rtm])
                    K.op("act", lam(lambda e, k, tm: e.activation(out=dst[:, k, tsl], in_=tm, func=AF.Identity,
                                                                   scale=sc[:, l, ia, k:k + 1], bias=sc[:, l, ib, k:k + 1]), k, tm),
                         reads=[Rtm, R_sc], writes=[R_dst[k][tb]])
            if run:
                for tb in range(4):
                    blk(tb)
                    if after_blk is not None:
                        after_blk(tb)
            return blk, bufs

        def mlp(l, final_store=None, tail_setup=None):
            A.off = OFF_R1
            h2 = A.alloc([128, 8, NT], BF16)
            R_h2 = [[Region(f"h2_{k}_{tb}") for tb in range(4)] for k in range(8)]
            A.off = OFF_R3
            uT = A.alloc([128, 8, NT], BF16)
            R_uT = [[Region(f"uT{c}_{tb}") for tb in range(4)] for c in range(8)]
            w1_off = A.off
            W1 = WStream(4, [128, 8, 256], "w1s")
            W2 = WStream(6, [128, 2, D], "w2s")
            w1v = w1[l].rearrange("(k p) f -> p k f", p=128)
            w2v = w2[l].rearrange("(c p) f -> p c f", p=128)
            cn = {"a": 0, "b": 0}

            def up(w1b, Rw1, cj, c, tb):
                bk = cn["a"] % 4
                cn["a"] += 1
                K.group("pe", [lam(lambda e, k: e.matmul(bank(bk), w1b[:, k, cj * 128:(cj + 1) * 128],
                                                          h2[:, k, tb * 512:(tb + 1) * 512], start=(k == 0), stop=(k == 7)), k)
                               for k in range(8)], reads=[Rw1] + [R_h2[k][tb] for k in range(8)], writes=[pb[bk]])
                usl = uT[:, c, tb * 512:(tb + 1) * 512]
                K.op("act", lambda e: e.activation(out=usl, in_=bank(bk), func=AF.Relu), reads=[pb[bk]], writes=[R_uT[c][tb]])
                sq_eng = "dve"
                K.op(sq_eng, lambda e: e.tensor_tensor(out=usl, in0=usl, in1=usl, op=ALU.mult),
                     reads=[R_uT[c][tb]], writes=[R_uT[c][tb]])

            def down(w2l, oc, tb):
                bk = 4 + cn["b"] % 4
                cn["b"] += 1
                K.group("pe", [lam(lambda e, c: e.matmul(bank(bk), w2l[c // 2][0][:, c % 2, oc * 128:(oc + 1) * 128],
                                                          uT[:, c, tb * 512:(tb + 1) * 512], start=(c == 0), stop=(c == 7)), c)
                               for c in range(8)],
                        reads=[w2l[j][1] for j in range(4)] + [R_uT[c][tb] for c in range(8)], writes=[pb[bk]])
                K.op("dve", lambda e: e.scalar_tensor_tensor(
                    out=xT[:, oc, tb * 512:(tb + 1) * 512], in0=bank(bk), scalar=sc[:, l, 7, oc:oc + 1],
                    in1=xT[:, oc, tb * 512:(tb + 1) * 512], op0=ALU.mult, op1=ALU.add),
                    reads=[pb[bk], R_sc, R_xT[oc][tb]], writes=[R_xT[oc][tb]])

            w1q0 = [W1.load(lambda buf, c2=c2: [(buf, w1v[:, :, c2 * 256:(c2 + 1) * 256])]) for c2 in range(4)]

            def q0_block(tb):
                for c2 in range(4):
                    for cj in range(2):
                        up(w1q0[c2][0], w1q0[c2][1], cj, c2 * 2 + cj, tb)
            _, nbufs = norm_mod(l, 4, 5, h2, R_h2, after_blk=q0_block)
            for q in range(4):
                for c2 in range(4):
                    if q == 0:
                        continue
                    f0 = q * 1024 + c2 * 256
                    w1b, Rw1 = W1.load(lambda buf: [(buf, w1v[:, :, f0:f0 + 256])])
                    for cj in range(2):
                        for tb in range(4):
                            up(w1b, Rw1, cj, c2 * 2 + cj, tb)
                w2l = []
                for c2 in range(4):
                    cc0 = q * 8 + c2 * 2
                    w2l.append(W2.load(lambda buf: [(buf, w2v[:, cc0:cc0 + 2, :])]))
                if q == 3 and tail_setup is not None:
                    tail_setup(w1_off, W1.regs, nbufs, h2, R_h2)
                if q == 3 and final_store is not None:
                    for tb in range(4):
                        for oc in range(8):
                            down(w2l, oc, tb)
                        final_store(tb)
                else:
                    for oc in range(8):
                        for tb in range(4):
                            down(w2l, oc, tb)
            K.barrier()

        L1MODES = ("l1mix", "full", "nocc")


        l1 = {}

        def l1_front_setup(w1_off, w1_regs, nbufs, h2, R_h2):
            l1["h1"], l1["R_h1"] = h2, R_h2
            _save = A.off
            A.off = w1_off
            l1["F256"] = A.alloc([128, 2, 512], BF16)
            l1["Zt"] = [A.alloc([128, 2, D], BF16) for _ in range(2)]
            A.off = _save
            l1["R_kf"] = Region("l1F256")
            l1["R_Zt"] = [Region("Zt0"), Region("Zt1")]
            l1["w1_regs"] = w1_regs
            K.dma("pool", lambda e: e.dma_start(out=l1["F256"], in_=dftc.rearrange("(k p) f -> p k f", p=128)),
                  writes=[l1["R_kf"]] + list(w1_regs))
            l1["R_zin"] = [Region(f"zin{t}") for t in range(16)]
            l1["R_zout"] = Region("zout")
            l1["cn"] = 0
            l1["blk"], _ = norm_mod(1, 8, 1, l1["h1"], l1["R_h1"], bufs=nbufs, run=False)

        def l1_front_blk(tb):
            l = 1
            h1, R_h1, F256, Zt, R_Zt = l1["h1"], l1["R_h1"], l1["F256"], l1["Zt"], l1["R_Zt"]
            R_zin = l1["R_zin"]
            zin_v = zin.ap().rearrange("(n c b) (m l) -> c n m b l", c=2, n=64, b=8, m=32, l=128)
            l1["blk"](tb)

            def chan_dft(t):
                zt, Rzt = Zt[t % 2], R_Zt[t % 2]
                extra = list(l1["w1_regs"]) if t < 2 else []

                def grp(g):
                    bk = l1["cn"] % 4
                    l1["cn"] += 1
                    K.group("pe", [lam(lambda e, kk: e.matmul(bank(bk), h1[:, 2 * g + kk, t * 128:(t + 1) * 128], F256[:, kk, :],
                                                               start=(kk == 0), stop=(kk == 1)), kk) for kk in range(2)],
                            reads=[l1["R_kf"]] + [R_h1[2 * g + kk][t // 4] for kk in range(2)], writes=[pb[bk]])
                    src_ = bank(bk).rearrange("p (c m) -> p c m", c=2)
                    dst = zt[:, :, g * 256:(g + 1) * 256]
                    if g % 2 == 0:
                        K.op("act", lambda e: e.activation(out=dst, in_=src_, func=AF.Identity), reads=[pb[bk]],
                             writes=[Rzt] + extra)
                    else:
                        K.op("dve", lambda e: e.tensor_copy(out=dst, in_=src_), reads=[pb[bk], Rzt], writes=[Rzt] + extra)
                for g in range(4):
                    grp(g)
                for c in range(2):
                    for rl in range(4):
                        K.dma("sp", lam(lambda e, c, rl: e.dma_start(
                            out=zin_v[c, 4 * t + rl],
                            in_=zt[rl * 32:(rl + 1) * 32, c, :].rearrange("p (b l) -> p b l", l=128)), c, rl),
                            reads=[Rzt], writes=[R_zin[t]], add=True)

            for t in range(4 * tb, 4 * tb + 4):
                chan_dft(t)
            K.special("pool", lambda e: e.collective_compute(
                "AllGather", ALU.bypass, replica_groups=[[0, 1], [2, 3], [4, 5], [6, 7]],
                ins=[zin.ap()[256 * tb:256 * (tb + 1), :]], outs=[zout.ap()[512 * tb:512 * (tb + 1), :]]),
                sems["cc"][tb], 1, reads=[R_zin[t] for t in range(4 * tb, 4 * tb + 4)], writes=[l1["R_zout"]], add=True)

        def fourier_back():
            l = 1
            R_zout = l1["R_zout"]
            A.off = OFF_R3
            WA = A.alloc([128, 64], BF16)
            MB = A.alloc([128, 32, 2, 64], BF16)
            FWP = A.alloc([128, 8, D], BF16)
            L1C_END = A.off
            R_k = Region("l1consts")
            K.dma("pool", lambda e: e.dma_start(out=WA, in_=wsta), writes=[R_k])
            K.dma("pool", lambda e: e.dma_start(out=MB[0:64], in_=mbc.rearrange("p (a b c) -> p a b c", a=32, b=2)),
                  writes=[R_k], add=True)
            K.dma("pool", lambda e: e.dma_start(out=FWP, in_=fw.rearrange("(k p) f -> p k f", p=128)),
                  writes=[R_k], add=True)
            A.off = OFF_R1
            FT = A.alloc([128, 8, NT], BF16)
            R_FT = [Region(f"FT{c}") for c in range(8)]
            A.off = L1C_END
            Zb = [A.alloc([128, 64, 128], BF16) for _ in range(2)]
            R_Zb = [Region("Zb0"), Region("Zb1")]
            Ab = [A.alloc([128, 128, 64], BF16)]
            R_Ab = [Region("Ab0")]
            zout_v = zout.ap().rearrange("(g r tn c b) (m l) -> r c b g tn m l", g=4, r=2, tn=16, c=2, b=8, m=32, l=128)

            def pos_dft(cb):
                zb, Rzb = Zb[cb % 2], R_Zb[cb % 2]
                ab, Rab = Ab[0][0:64], R_Ab[0]
                first = True
                for r in range(2):
                    for c in range(2):
                        for g in range(4):
                            K.dma("sp", lam(lambda e, r, c, g: e.dma_start(
                                out=zb[c * 64 + 16 * g:c * 64 + 16 * g + 16, r * 32:(r + 1) * 32, :],
                                in_=zout_v[r, c, cb, g]), r, c, g),
                                reads=[R_zout], writes=[Rzb], add=not first)
                            first = False
                def stA(b8):
                    bk = b8 % 4
                    K.group("pe", [lam(lambda e, j: e.matmul(bank(bk)[0:64, j * 64:(j + 1) * 64], zb[:, :, 8 * b8 + j], WA,
                                                              start=True, stop=True), j) for j in range(8)],
                            reads=[Rzb, R_k], writes=[pb[bk]])
                    dst = ab[:, 8 * b8:8 * b8 + 8, :]
                    srcp = bank(bk)[0:64, :].rearrange("p (j c) -> p j c", c=64)
                    if b8 % 2 == 0:
                        K.op("act", lambda e: e.activation(out=dst, in_=srcp, func=AF.Identity), reads=[pb[bk]], writes=[Rab])
                    else:
                        K.op("dve", lambda e: e.tensor_copy(out=dst, in_=srcp), reads=[pb[bk], Rab], writes=[Rab])
                for b8 in range(16):
                    stA(b8)

                def stB(k8):
                    bk = 4 + k8 % 4
                    fns = []
                    for j in range(8):
                        k1l = 8 * k8 + j
                        for c2 in range(2):
                            fns.append(lam(lambda e, j, k1l, c2: e.matmul(
                                bank(bk)[:, j * 64:(j + 1) * 64], ab[:, :, c2 * 32 + k1l], MB[0:64, k1l, c2, :],
                                start=(c2 == 0), stop=(c2 == 1)), j, k1l, c2))
                    K.group("pe", fns, reads=[Rab, R_k], writes=[pb[bk]])
                    dst = FT[:, cb, 512 * k8:512 * (k8 + 1)]
                    if k8 % 2 == 0:
                        K.op("act", lambda e: e.activation(out=dst, in_=bank(bk), func=AF.Identity), reads=[pb[bk]], writes=[R_FT[cb]])
                    else:
                        K.op("dve", lambda e: e.tensor_copy(out=dst, in_=bank(bk)), reads=[pb[bk], R_FT[cb]], writes=[R_FT[cb]])
                for k8 in range(4):
                    stB(k8)

            for cb in range(8):
                pos_dft(cb)
            cw = {"n": 0}
            xTv = xT.rearrange("p k (r c) -> p k c r", c=32)

            def fw_block(j, oc):
                bk = cw["n"] % 4
                cw["n"] += 1
                K.group("pe", [lam(lambda e, q: e.matmul(bank(bk), FWP[:, q, oc * 128:(oc + 1) * 128],
                                                          FT[:, q, 512 * j:512 * (j + 1)],
                                                          start=(q == 0), stop=(q == 7)), q) for q in range(8)],
                        reads=[R_k] + R_FT, writes=[pb[bk]])
                xv = xTv[:, oc, 8 * j:8 * j + 8, :]
                K.op("dve", lambda e: e.scalar_tensor_tensor(
                    out=xv, in0=bank(bk).rearrange("p (c r) -> p c r", c=8), scalar=sc[:, l, 6, oc:oc + 1], in1=xv,
                    op0=ALU.mult, op1=ALU.add),
                    reads=[pb[bk], R_sc] + [R_xT[oc][tb] for tb in range(4)], writes=[R_xT[oc][tb] for tb in range(4)])

            for j in range(4):
                for oc in range(8):
                    fw_block(j, oc)
            K.barrier()

        if mode in L1MODES:
            mlp(0, final_store=l1_front_blk, tail_setup=l1_front_setup)
            fourier_back()
        elif mode != "attn_only":
            mlp(0)
        A.off = OFF_R1
        ot = [A.alloc([128, D], F32) for _ in range(2)]
        R_ot = [Region("ot0"), Region("ot1")]

        def store_tile(t):
            b0 = 2 * (t % 2)
            pst = bank2(b0)
            K.group("pe", [lam(lambda e, k: e.transpose(pst[:, k * 128:(k + 1) * 128], xT[:, k, t * 128:(t + 1) * 128], ident_f), k)
                           for k in range(8)],
                    reads=[R_const] + [R_xT[k][t // 4] for k in range(8)], writes=[pb[b0], pb[b0 + 1]])
            o_, Ro = ot[t % 2], R_ot[t % 2]
            if t % 2 == 0:
                K.op("act", lambda e: e.activation(out=o_, in_=pst, func=AF.Identity), reads=[pb[b0], pb[b0 + 1]], writes=[Ro])
            else:
                K.op("dve", lambda e: e.tensor_copy(out=o_, in_=pst), reads=[pb[b0], pb[b0 + 1]], writes=[Ro])
            K.dma("sp", lambda e: e.dma_start(out=out[t * 128:(t + 1) * 128, :], in_=o_), reads=[Ro])

        def store_block(tb):
            for t in range(4 * tb, 4 * tb + 4):
                store_tile(t)
        if mode == "full":
            mlp(1, final_store=store_block)
        else:
            for tb in range(4):
                store_block(tb)

        final = [(K.dsem["sp"][i], v) for i, v in enumerate(K.dval["sp"]) if v]
        K.ops["sp"].append((final, None, None))
        K.emit()
    return nc


def _bias_tables(rpb, s):
    own_cols = np.arange(32) + 32 * s
    halo_cols = np.arange(32, 40) if s == 0 else np.arange(24, 32)
    tab = np.full((16, 10, 128, 128), NEG, np.float32)

    def fill(ci, qrows, krows, kcols):
        kr = np.repeat(krows, len(kcols))
        kc = np.tile(kcols, len(krows))
        qr = np.repeat(qrows, 32)
        qc = np.tile(own_cols, len(qrows))
        rs = np.clip(qr - 4, 0, 56)
        cs = np.clip(qc - 8, 0, 48)
        ok = (kr[:, None] >= rs[None, :]) & (kr[:, None] < rs[None, :] + 8) & \
             (kc[:, None] >= cs[None, :]) & (kc[:, None] < cs[None, :] + 16)
        dr = np.clip(kr[:, None] - qr[None, :] + 7, 0, 14)
        dc = np.clip(kc[:, None] - qc[None, :] + 15, 0, 30)
        vals = rpb[:, dr, dc]
        blk = np.where(ok[None], vals, np.float32(NEG)).astype(np.float32)
        tab[:, ci, :blk.shape[1], :] = blk

    def tile_chunks(i, c0):
        qrows = np.arange(4 * i, 4 * i + 4)
        own = [g for g in (i - 1, i, i + 1) if 0 <= g < 16]
        for j, g in enumerate(own):
            fill(c0 + j, qrows, np.arange(4 * g, 4 * g + 4), own_cols)
        r0, r1 = max(4 * i - 4, 0), min(4 * i + 8, 64)
        fill(c0 + len(own), qrows, np.arange(r0, r1), halo_cols)

    tile_chunks(0, 0)
    tile_chunks(5, 3)
    tile_chunks(15, 7)
    return np.ascontiguousarray(tab.transpose(0, 2, 1, 3).reshape(16, 128, 1280))


def _consts():
    c = np.zeros((128, 256), np.float32)
    c[:, 0:128] = np.eye(128, dtype=np.float32)
    bo = np.zeros((128, 128), np.float32)
    bo[:64, :64] = 1.0 / 64.0
    bo[64:, 64:] = 1.0 / 64.0
    c[:, 128:256] = bo
    return c


def _dft_consts(s):
    j = np.arange(256, dtype=np.float64)
    ang = 2 * np.pi * np.outer(j, j) / 256.0
    dftc = np.concatenate([np.cos(ang), np.sin(ang)], 1).astype(np.float32)
    n1 = np.arange(64, dtype=np.float64)
    k1 = 32 * s + np.arange(32, dtype=np.float64)
    th = 2 * np.pi * np.outer(n1, k1) / 64.0
    wsta = np.zeros((2, 64, 2, 32), np.float64)
    wsta[0, :, 0], wsta[0, :, 1] = np.cos(th), np.sin(th)
    wsta[1, :, 0], wsta[1, :, 1] = -np.sin(th), np.cos(th)
    n2 = np.arange(64, dtype=np.float64)
    k2 = np.arange(64, dtype=np.float64)
    ph = 2 * np.pi * (n2[:, None, None] * k2[None, None, :] / 64.0 + n2[:, None, None] * k1[None, :, None] / 4096.0)
    mb = np.zeros((64, 32, 2, 64), np.float64)
    mb[:, :, 0, :] = np.cos(ph) / 1024.0
    mb[:, :, 1, :] = -np.sin(ph) / 1024.0
    return dftc, wsta.reshape(128, 64).astype(np.float32), mb.reshape(64, 32 * 2 * 64).astype(np.float32)


_NC_CACHE = {}


def _get_nc(mode):
    if mode not in _NC_CACHE:
        _NC_CACHE[mode] = build_program(mode)
    return _NC_CACHE[mode]


def kernel(x, c, ctx, c_ctx, ada_w, ada_b, norm_g, attn_wqkv, attn_wo, q_norm_g, k_norm_g, rpb,
           fourier_w, mlp_w1, mlp_w2, _mode="full"):
    f = lambda a: np.ascontiguousarray(np.asarray(a, dtype=np.float32))
    x, c, ctx, c_ctx = f(x), f(c), f(ctx), f(c_ctx)
    ada_w, ada_b, norm_g = f(ada_w), f(ada_b), f(norm_g)
    wqkv, wo, fw = f(attn_wqkv)[0], f(attn_wo)[0], f(fourier_w)[0]
    gq, gk = f(q_norm_g)[0], f(k_norm_g)[0]
    rpb0 = f(rpb)[0]
    w1, w2 = f(mlp_w1), f(mlp_w2)
    cm = _consts()
    nc = _get_nc(_mode)
    in_maps = []
    xg = x.reshape(4, 64, 64, D)
    tabs = [_bias_tables(rpb0, s) for s in range(2)]
    dfts = [_dft_consts(s) for s in range(2)]
    for core in range(8):
        b, s = core // 2, core % 2
        own = xg[b][:, 32 * s:32 * s + 32, :].reshape(NT, D)
        hal = (xg[b][:, 32:40, :] if s == 0 else xg[b][:, 24:32, :]).reshape(512, D)
        sv = np.zeros((256, 128), np.float32)
        sv[0:96] = ada_b.reshape(96, 128)
        sv[128:136] = c[b].reshape(8, 128)
        sv[136:144] = c_ctx.reshape(8, 128)
        sv[144:176] = norm_g.reshape(32, 128)
        sv[176, 0:64] = gq
        sv[176, 64:128] = gq
        sv[177, 0:64] = gk
        sv[177, 64:128] = gk
        in_maps.append({
            "xown": np.ascontiguousarray(own), "xhalo": np.ascontiguousarray(hal), "ctxb": ctx[b],
            "smallv": sv, "ada_w": ada_w, "wqkv": wqkv, "wo": wo, "fw": fw, "w1": w1, "w2": w2,
            "btab": tabs[s], "cmat": cm, "dftc": dfts[s][0], "wsta": dfts[s][1], "mbc": dfts[s][2],
        })
    res = run_bass_kernel_spmd(nc, in_maps, core_ids=list(range(8)))
    outp = np.empty((4, 64, 64, D), np.float32)
    for core in range(8):
        b, s = core // 2, core % 2
        outp[b][:, 32 * s:32 * s + 32, :] = res.results[core]["out"].reshape(64, 32, D)
    return outp.reshape(4, 4096, D)
```
